# Optimizing a Trainium2 kernel written in Bass

```python
import jax, jax.numpy as jnp
from jax import lax
import numpy as np

D_MODEL = 2048
BATCH = 4
SEQ = 2048
DEPTH = 2
DEC_BATCH = 128
DEC_SEQ = 8
PAST_LEN = 16384
PAGE_SIZE = 128

D_CONV = D_MODEL // 2
CONV_A_WIDTH = 31
D_LRU = D_MODEL
LRU_HEADS = 16
LRU_HEAD_DIM = D_LRU // LRU_HEADS
CONV_B_WIDTH = 4
LRU_C = 8.0
D_FF = 3 * D_MODEL
FFN_CONV_WIDTH = 3
D_PLE = 256
LN_EPS = 1e-5
ALPHA = (2.0 * DEPTH) ** 0.25
BETA = (8.0 * DEPTH) ** -0.25
D_IN_TOTAL = 2 * D_CONV + 2 * D_LRU + 2 * D_MODEL
SPLIT_IDX = (D_CONV, 2 * D_CONV, 2 * D_CONV + D_LRU, 2 * D_CONV + 2 * D_LRU, 2 * D_CONV + 2 * D_LRU + D_MODEL)

kernel_name = "hybrid_conformer_rglru_decoder_step"


def layer_norm(x, g, b):
    xf = x.astype(jnp.float32)
    mu = jnp.mean(xf, axis=-1, keepdims=True)
    var = jnp.mean(jnp.square(xf - mu), axis=-1, keepdims=True)
    y = (xf - mu) * lax.rsqrt(var + LN_EPS) * g.astype(jnp.float32) + b.astype(jnp.float32)
    return y.astype(x.dtype)


def causal_dwconv(x, prev, w, b):
    k, c = w.shape
    xp = jnp.concatenate([prev.astype(x.dtype), x], axis=1)
    y = lax.conv_general_dilated(xp, w.astype(x.dtype)[:, None, :], window_strides=(1,), padding="VALID",
                                 dimension_numbers=("NWC", "WIO", "NWC"), feature_group_count=c)
    return y + b.astype(x.dtype), xp[:, xp.shape[1] - (k - 1):]


def _lin_combine(left, right):
    a1, b1 = left
    a2, b2 = right
    return a1 * a2, a2 * b1 + b2


def rg_lru(x, h0, w_r, b_r, w_i, b_i, lam, reset_first):
    bsz, t, c = x.shape
    xh = x.reshape(bsz, t, LRU_HEADS, LRU_HEAD_DIM)
    r = jax.nn.sigmoid(jnp.einsum("bthi,hij->bthj", xh, w_r).reshape(bsz, t, c) + b_r).astype(jnp.float32)
    gi = jax.nn.sigmoid(jnp.einsum("bthi,hij->bthj", xh, w_i).reshape(bsz, t, c) + b_i).astype(jnp.float32)
    log_a = -LRU_C * r * jax.nn.softplus(-lam.astype(jnp.float32))
    a = jnp.exp(log_a)
    mult = jnp.sqrt(-jnp.expm1(2.0 * log_a))
    if reset_first:
        mult = mult.at[:, 0].set(1.0)
    u = mult * gi * x.astype(jnp.float32)
    u = u.at[:, 0].add(a[:, 0] * h0.astype(jnp.float32))
    _, h = lax.associative_scan(_lin_combine, (a, u), axis=1)
    return h.astype(x.dtype), h[:, -1].astype(x.dtype)


def decoder_layer(x, ple, buf_a, buf_b, h0, buf_f, reset_first,
                  w_in, conv_a_w, conv_a_b, ln_a_g, ln_a_b, w_a_out,
                  conv_b_w, conv_b_b, w_r, b_r, w_i, b_i, lru_lambda, w_b_out,
                  w_o, ln1_g, ln1_b, w_up, ffn_conv_w, ffn_conv_b, w_down, ln2_g, ln2_b,
                  w_pe, w_pg, b_pg, ln3_g, ln3_b):
    z = x @ w_in
    a_val, a_gate, b_x, b_gate, g_a, g_b = jnp.split(z, SPLIT_IDX, axis=-1)
    u = a_val * jax.nn.sigmoid(a_gate)
    ca, new_a = causal_dwconv(u, buf_a, conv_a_w, conv_a_b)
    out_a = jax.nn.silu(layer_norm(ca, ln_a_g, ln_a_b)) @ w_a_out
    cb, new_b = causal_dwconv(b_x, buf_b, conv_b_w, conv_b_b)
    hb, h_last = rg_lru(cb, h0, w_r, b_r, w_i, b_i, lru_lambda, reset_first)
    out_b = (hb * jax.nn.gelu(b_gate)) @ w_b_out
    merged = jax.nn.sigmoid(g_a) * out_a + jax.nn.sigmoid(g_b) * out_b
    x = layer_norm(ALPHA * x + merged @ w_o, ln1_g, ln1_b)
    up = x @ w_up
    fu, fg = jnp.split(up, [D_FF], axis=-1)
    fgc, new_f = causal_dwconv(fg, buf_f, ffn_conv_w, ffn_conv_b)
    x = layer_norm(ALPHA * x + (jax.nn.gelu(fgc) * fu) @ w_down, ln2_g, ln2_b)
    e = jax.nn.sigmoid(x @ w_pg + b_pg) * (ple.astype(x.dtype) @ w_pe)
    x = layer_norm(ALPHA * x + e, ln3_g, ln3_b)
    return x, new_a, new_b, h_last, new_f


def setup_inputs(seed: int = 0) -> dict:
    key = jax.random.key(seed)
    ks = iter(jax.random.split(key, 48))
    f32 = jnp.float32
    nrm = lambda shape, s: jax.random.normal(next(ks), shape, f32) * s
    gain = lambda shape: 1.0 + nrm(shape, 0.02)
    L = DEPTH
    s_lam = jax.random.uniform(next(ks), (L, D_LRU), f32, 0.9, 0.999) ** (1.0 / LRU_C)
    lru_lambda = jnp.log(s_lam) - jnp.log1p(-s_lam)
    return {
        "x_prompt": nrm((BATCH, SEQ, D_MODEL), 1.0),
        "x_sample": nrm((DEC_BATCH, DEC_SEQ, D_MODEL), 1.0),
        "state_conv_a": nrm((L, DEC_BATCH, CONV_A_WIDTH - 1, D_CONV), 0.5),
        "state_conv_b": nrm((L, DEC_BATCH, CONV_B_WIDTH - 1, D_LRU), 1.0),
        "state_rglru": nrm((L, DEC_BATCH, D_LRU), 0.5),
        "state_conv_ffn": nrm((L, DEC_BATCH, FFN_CONV_WIDTH - 1, D_FF), 1.0),
        "p_prompt": nrm((L, BATCH, SEQ, D_PLE), 1.0),
        "p_sample": nrm((L, DEC_BATCH, DEC_SEQ, D_PLE), 1.0),
        "w_in": nrm((L, D_MODEL, D_IN_TOTAL), D_MODEL ** -0.5),
        "conv_a_w": nrm((L, CONV_A_WIDTH, D_CONV), CONV_A_WIDTH ** -0.5),
        "conv_a_b": nrm((L, D_CONV), 0.02),
        "ln_a_g": gain((L, D_CONV)),
        "ln_a_b": nrm((L, D_CONV), 0.02),
        "w_a_out": nrm((L, D_CONV, D_MODEL), D_CONV ** -0.5),
        "conv_b_w": nrm((L, CONV_B_WIDTH, D_LRU), CONV_B_WIDTH ** -0.5),
        "conv_b_b": nrm((L, D_LRU), 0.02),
        "w_r": nrm((L, LRU_HEADS, LRU_HEAD_DIM, LRU_HEAD_DIM), LRU_HEAD_DIM ** -0.5),
        "b_r": nrm((L, D_LRU), 0.02),
        "w_i": nrm((L, LRU_HEADS, LRU_HEAD_DIM, LRU_HEAD_DIM), LRU_HEAD_DIM ** -0.5),
        "b_i": nrm((L, D_LRU), 0.02),
        "lru_lambda": lru_lambda,
        "w_b_out": nrm((L, D_LRU, D_MODEL), D_LRU ** -0.5),
        "w_o": nrm((L, D_MODEL, D_MODEL), BETA * D_MODEL ** -0.5),
        "ln1_g": gain((L, D_MODEL)),
        "ln1_b": nrm((L, D_MODEL), 0.02),
        "w_up": nrm((L, D_MODEL, 2 * D_FF), D_MODEL ** -0.5),
        "ffn_conv_w": nrm((L, FFN_CONV_WIDTH, D_FF), FFN_CONV_WIDTH ** -0.5),
        "ffn_conv_b": nrm((L, D_FF), 0.02),
        "w_down": nrm((L, D_FF, D_MODEL), BETA * D_FF ** -0.5),
        "ln2_g": gain((L, D_MODEL)),
        "ln2_b": nrm((L, D_MODEL), 0.02),
        "w_pe": nrm((L, D_PLE, D_MODEL), BETA * D_PLE ** -0.5),
        "w_pg": nrm((L, D_MODEL, D_MODEL), D_MODEL ** -0.5),
        "b_pg": nrm((L, D_MODEL), 0.02),
        "ln3_g": gain((L, D_MODEL)),
        "ln3_b": nrm((L, D_MODEL), 0.02),
    }


def reference(x_prompt, x_sample, state_conv_a, state_conv_b, state_rglru, state_conv_ffn, p_prompt, p_sample,
              w_in, conv_a_w, conv_a_b, ln_a_g, ln_a_b, w_a_out, conv_b_w, conv_b_b, w_r, b_r, w_i, b_i,
              lru_lambda, w_b_out, w_o, ln1_g, ln1_b, w_up, ffn_conv_w, ffn_conv_b, w_down, ln2_g, ln2_b,
              w_pe, w_pg, b_pg, ln3_g, ln3_b):
    dt = x_prompt.dtype
    xp, xs = x_prompt, x_sample
    pa, pb, ph, pf = [], [], [], []
    sa, sb, sh, sf = [], [], [], []
    for l in range(DEPTH):
        wl = (w_in[l], conv_a_w[l], conv_a_b[l], ln_a_g[l], ln_a_b[l], w_a_out[l],
              conv_b_w[l], conv_b_b[l], w_r[l], b_r[l], w_i[l], b_i[l], lru_lambda[l], w_b_out[l],
              w_o[l], ln1_g[l], ln1_b[l], w_up[l], ffn_conv_w[l], ffn_conv_b[l], w_down[l], ln2_g[l], ln2_b[l],
              w_pe[l], w_pg[l], b_pg[l], ln3_g[l], ln3_b[l])
        bsz = xp.shape[0]
        xp, na, nb, nh, nf = decoder_layer(
            xp, p_prompt[l],
            jnp.zeros((bsz, CONV_A_WIDTH - 1, D_CONV), dt), jnp.zeros((bsz, CONV_B_WIDTH - 1, D_LRU), dt),
            jnp.zeros((bsz, D_LRU), dt), jnp.zeros((bsz, FFN_CONV_WIDTH - 1, D_FF), dt), True, *wl)
        pa.append(na); pb.append(nb); ph.append(nh); pf.append(nf)
        xs, na, nb, nh, nf = decoder_layer(
            xs, p_sample[l], state_conv_a[l], state_conv_b[l], state_rglru[l], state_conv_ffn[l], False, *wl)
        sa.append(na); sb.append(nb); sh.append(nh); sf.append(nf)
    return (xp, xs,
            jnp.stack(pa), jnp.stack(pb), jnp.stack(ph), jnp.stack(pf),
            jnp.stack(sa), jnp.stack(sb), jnp.stack(sh), jnp.stack(sf))
```

```python
import contextlib
import numpy as np
import concourse.bass as bass
import concourse.mybir as mybir
from concourse.bass_utils import run_bass_kernel_spmd

F32 = mybir.dt.float32
BF16 = mybir.dt.bfloat16
AF = mybir.ActivationFunctionType
ALU = mybir.AluOpType

D = 2048
DC = 1024
DFF = 6144
DPLE = 256
L = 2
TP = 512
NS = 4
TS = NS * 8
T = TP + TS
NTILE = 4
ALPHA = (2.0 * L) ** 0.25
EPS = 1e-5
NCORES = 8
PROMPT_CORES = [0, 1, 4, 5]

_off = {}
_n = 0
for _name, _cnt in [("caw", 8 * 31), ("cab", 8), ("lag", 8), ("lab", 8), ("cbw", 16 * 4), ("cbb", 16), ("br", 16),
                    ("bi", 16), ("lam", 16), ("l1g", 16), ("l1b", 16), ("fcw", 48 * 3), ("fcb", 48), ("l2g", 16),
                    ("l2b", 16), ("bpg", 16), ("l3g", 16), ("l3b", 16)]:
    _off[_name] = _n
    _n += _cnt
NV = _n

ENG = ["pe", "act", "dve", "pool", "sp"]


class Sched:
    def __init__(self):
        self.ops = {e: [] for e in ENG}
        self.last_w = {}
        self.readers = {}
        self.slot_cnt = {}
        self.slot_last = {}
        self.epoch = 0

    def add(self, eng, fn, r=(), w=(), slot=None):
        deps = set()
        for k in tuple(r) + tuple(w):
            lw = self.last_w.get(k)
            if lw is not None:
                deps.add(lw)
        for k in w:
            for rd in self.readers.get(k, ()):
                deps.add(rd)
        if slot is not None and slot in self.slot_last:
            deps.add(self.slot_last[slot])
        idx = len(self.ops[eng])
        if slot is not None:
            n = self.slot_cnt.get(slot, 0) + 1
            self.slot_cnt[slot] = n
            ident = ("dma", slot, n)
            self.slot_last[slot] = ident
        else:
            ident = ("op", eng, idx)
        self.ops[eng].append(dict(fn=fn, deps=deps, epoch=self.epoch, ident=ident, slot=slot, ms=False, msn=0))
        for k in w:
            self.last_w[k] = ident
            self.readers[k] = set()
        for k in r:
            self.readers.setdefault(k, set()).add(ident)
        return ident

    def emit(self, nc, stack, final_slots):
        for e in ENG:
            for op in self.ops[e]:
                for d in op["deps"]:
                    if d[0] == "op":
                        if e == "pe" and d[1] == "pe":
                            continue
                        self.ops[d[1]][d[2]]["ms"] = True
        sems = {}
        for e in ENG:
            cnt = {}
            for op in self.ops[e]:
                if op["ms"] and op["slot"] is None:
                    ep = op["epoch"]
                    cnt[ep] = cnt.get(ep, 0) + 1
                    op["msn"] = cnt[ep]
                    if (e, ep) not in sems:
                        sems[(e, ep)] = stack.enter_context(nc.semaphore(f"s_{e}_{ep}"))
        slot_sems = {s: stack.enter_context(nc.semaphore(f"d_{s}")) for s in self.slot_cnt}
        block = stack.enter_context(nc.Block())
        ops = self.ops
        slot_cnt = self.slot_cnt

        def run(e, eng):
            waited = {}
            for op in ops[e]:
                need = {}
                for d in op["deps"]:
                    if d[0] == "dma":
                        key = ("d", d[1])
                        sem = slot_sems[d[1]]
                        val = 16 * d[2]
                    else:
                        if e == "pe" and d[1] == "pe":
                            continue
                        p = ops[d[1]][d[2]]
                        key = (d[1], p["epoch"])
                        sem = sems[key]
                        val = p["msn"]
                    if key not in need or need[key][1] < val:
                        need[key] = (sem, val)
                for key, (sem, val) in need.items():
                    if waited.get(key, 0) >= val:
                        continue
                    eng.wait_ge(sem, val)
                    waited[key] = val
                ins = op["fn"](eng)
                if op["slot"] is not None:
                    ins.then_inc(slot_sems[op["slot"]], 16)
                elif op["ms"]:
                    ins.then_inc(sems[(e, op["epoch"])], 1)
            if e == "sp":
                for s in final_slots:
                    eng.wait_ge(slot_sems[s], 16 * slot_cnt[s])

        @block.tensor
        def _(t):
            run("pe", t)

        @block.scalar
        def _(t):
            run("act", t)

        @block.vector
        def _(t):
            run("dve", t)

        @block.gpsimd
        def _(t):
            run("pool", t)

        @block.sync
        def _(t):
            run("sp", t)


def build_nc(ntile=NTILE, nlayer=L):
    nc = bass.Bass("TRN2", target_bir_lowering=False)
    S = Sched()
    stack = contextlib.ExitStack()

    def din(name, shape):
        return nc.dram_tensor(name, list(shape), F32, kind="ExternalInput").ap()

    def dout(name, shape):
        return nc.dram_tensor(name, list(shape), F32, kind="ExternalOutput").ap()

    xpT = din("xpT", [128, 16, 2048])
    xsT = din("xsT", [128, 16, 128])
    ppT = din("ppT", [L, 128, 2, 2048])
    psT = din("psT", [L, 128, 2, 128])
    scaT = din("scaT", [L, 128, 8, 16, 30])
    scbT = din("scbT", [L, 128, 16, 16, 3])
    srgT = din("srgT", [L, 128, 16, 16])
    scfT = din("scfT", [L, 128, 48, 16, 2])
    prm = din("prm", [L, 128, NV])
    W_in = din("W_in", [L, 80, 128, 16 * 128])
    W_aout = din("W_aout", [L, 16, 128, 8 * 128])
    W_bout = din("W_bout", [L, 16, 128, 16 * 128])
    W_o = din("W_o", [L, 16, 128, 16 * 128])
    W_up = din("W_up", [L, 96, 128, 16 * 128])
    W_down = din("W_down", [L, 32, 128, 24 * 128])
    W_pg = din("W_pg", [L, 16, 128, 16 * 128])
    W_pe = din("W_pe", [L, 16, 128, 2 * 128])
    W_gate = din("W_gate", [L, 2, 128, 16 * 128])
    identd = din("ident", [128, 128])

    ypT = dout("ypT", [128, 16, 2048])
    ysT = dout("ysT", [128, 16, 128])
    oap = dout("oap", [L, 128, 8, 30])
    obp = dout("obp", [L, 128, 16, 3])
    ohp = dout("ohp", [L, 128, 16])
    ofp = dout("ofp", [L, 128, 48, 2])
    oas = dout("oas", [L, 128, 8, 16, 30])
    obs = dout("obs", [L, 128, 16, 16, 3])
    ohs = dout("ohs", [L, 128, 16, 16])
    ofs = dout("ofs", [L, 128, 48, 16, 2])

    def sb(name, shape, dt=F32):
        return stack.enter_context(nc.sbuf_tensor(name, list(shape), dt))

    xres = sb("xres", [128, 16, T])
    xb = sb("xb", [128, 16, T], BF16)
    work = sb("work", [128, 48, T], BF16)
    histp = [sb(f"histp{i}", [128, 30 + TP], BF16) for i in range(2)]
    hists = [sb(f"hists{i}", [128, NS, 38], BF16) for i in range(2)]
    dg = sb("dg", [128, 31, 128], BF16)
    dgb = [sb(f"dgb{i}", [128, 4, 128], BF16) for i in range(2)]
    NB = 4
    ring = [sb(f"ring{i}", [128, 24 * 128], BF16) for i in range(NB)]
    prm_t = sb("prm_t", [128, L, NV])
    wgate = sb("wgate", [128, 2, 2048], BF16)
    nsp8 = sb("nsp8", [128, L, 16])
    identb = sb("identb", [128, 128], BF16)
    onesb = sb("onesb", [128, 128], BF16)
    car_a = sb("car_a", [128, L, 8, 30])
    car_b = sb("car_b", [128, L, 16, 3])
    car_h = sb("car_h", [128, L, 16])
    car_f = sb("car_f", [128, L, 48, 2])
    stg_a = sb("stg_a", [128, 8, NS, 30])
    stg_b = sb("stg_b", [128, 16, NS, 3])
    stg_h = sb("stg_h", [128, 16, NS])
    stg_f = sb("stg_f", [128, 48, NS, 2])
    os_a = sb("os_a", [128, 8, NS, 8])
    os_b = sb("os_b", [128, 16, NS, 3])
    os_h = sb("os_h", [128, 16, NS])
    os_f = sb("os_f", [128, 48, NS, 2])
    pb = sb("pb", [128, 2, T], BF16)
    NTMP = 13
    TW = 568
    tmp = [sb(f"tmp{i}", [128, TW]) for i in range(NTMP)]
    ca32f = work[:, 24:40, :].rearrange("p a t -> p (a t)").bitcast(F32)

    class _CA:
        def __getitem__(self, idx):
            _, c, sl = idx
            return ca32f[:, c * T:(c + 1) * T][:, sl]
    ca32 = _CA()

    def cak(c):
        return [f"work{24 + 2 * c}", f"work{25 + 2 * c}"]
    psum = [stack.enter_context(nc.psum_tensor(f"ps{i}", [128, 2, 512], F32)) for i in range(4)]

    def psP(i):
        return psum[i][:, 0, 0:TP]

    def psS(i):
        return psum[i][:, 1, 0:TS]

    def tf(i, n=T):
        return tmp[i][:, 0:n]

    def tb(i, n=T, off=0):
        return tmp[i][:, :].bitcast(BF16)[:, off:off + n]

    tk = [f"tmp{i}" for i in range(NTMP)]

    ring_i = [0]
    ps_allowed = [[0, 1, 2, 3]]
    ps_i = [0]

    def next_ps():
        al = ps_allowed[0]
        s = al[ps_i[0] % len(al)]
        ps_i[0] += 1
        return s

    def load_w(src_ap, ncols):
        s = ring_i[0] % NB
        ring_i[0] += 1
        key = f"ring{s}"
        dst = ring[s][:, 0:ncols]
        S.add("pool", lambda e, dst=dst, src_ap=src_ap: e.dma_start(out=dst, in_=src_ap, max_dma_last_dim=8192),
              w=[key], slot=key)
        return s, key

    def mm(ps, lhs_fn, rhs_fn, nk, rkeys, first=True, last=True):
        def fn(e):
            ins = None
            for k in range(nk):
                st = first and k == 0
                sp_ = last and k == nk - 1
                rhs = rhs_fn(k)
                e.matmul(psP(ps), lhs_fn(k), rhs[:, 0:TP], start=st, stop=sp_)
                ins = e.matmul(psS(ps), lhs_fn(k), rhs[:, TP:T], start=st, stop=sp_)
            return ins
        S.add("pe", fn, r=rkeys, w=[f"ps{ps}"])

    def mm_kouter(groups, rhs_fn, nk, xkeys):
        for k in range(nk):
            def fn(e, k=k):
                ins = None
                rhs = rhs_fn(k)
                for (ps, lhs_fn, _) in groups:
                    e.matmul(psP(ps), lhs_fn(k), rhs[:, 0:TP], start=(k == 0), stop=(k == nk - 1))
                    ins = e.matmul(psS(ps), lhs_fn(k), rhs[:, TP:T], start=(k == 0), stop=(k == nk - 1))
                return ins
            S.add("pe", fn, r=[g[2] for g in groups] + [xkeys[k]], w=[f"ps{g[0]}" for g in groups])

    def act2(ps, outP, outS, func, wkeys, rkeys=(), s3=False, **kw):
        def fn(e):
            e.activation(out=outP, in_=psP(ps), func=func, **kw)
            inS = psS(ps)
            if s3:
                inS = inS.rearrange("p (s j) -> p s j", j=8)
            return e.activation(out=outS, in_=inS, func=func, **kw)
        S.add("act", fn, r=[f"ps{ps}"] + list(rkeys), w=wkeys)

    def act1(out, in_, func, rkeys, wkeys, **kw):
        S.add("act", lambda e: e.activation(out=out, in_=in_, func=func, **kw), r=rkeys, w=wkeys)

    def dve(fn, rkeys, wkeys):
        S.add("dve", fn, r=rkeys, w=wkeys)

    def s4(ap):
        return ap.rearrange("p (s j) -> p s j", j=8)

    S.add("sp", lambda e: e.dma_start(out=prm_t[:, :, :], in_=prm.rearrange("l p v -> p l v")), w=["prm"], slot="prm")
    S.add("pool", lambda e: e.dma_start(out=identb[:, :], in_=identd[:, :]), w=["identb"], slot="ident")
    dve(lambda e: e.memset(onesb[:, :], 1.0), [], ["onesb"])
    dve(lambda e: e.memset(car_a[:, :, :, :], 0.0), [], ["car_a"])
    dve(lambda e: e.memset(car_b[:, :, :, :], 0.0), [], ["car_b"])
    dve(lambda e: e.memset(car_h[:, :, :], 0.0), [], ["car_h"])
    dve(lambda e: e.memset(car_f[:, :, :, :], 0.0), [], ["car_f"])
    for l in range(nlayer):
        lam = prm_t[:, l, _off["lam"]:_off["lam"] + 16]
        act1(tf(0, 16), lam, AF.Exp, ["prm"], [tk[0]], scale=-1.0)
        act1(tf(1, 16), tf(0, 16), AF.Ln, [tk[0]], [tk[1]], bias=1.0)
        dve(lambda e, l=l: e.tensor_scalar(out=nsp8[:, l, :], in0=tf(1, 16), scalar1=-8.0, scalar2=None, op0=ALU.mult),
            [tk[1]], ["nsp8"])

    def pcol(l, name, i):
        o = _off[name] + i
        return prm_t[:, l, o:o + 1]

    class LN:
        def __init__(self, nch, src_fn, src_key_fn):
            self.nch = nch
            self.src_fn = src_fn
            self.src_key_fn = src_key_fn
            self.pending = None
            self.count = 0

        def feed(self, m):
            ti = 3 + (self.count % 2)
            vb = tb(ti, T, 0)
            vs = tb(ti, T, T)
            src = self.src_fn(m)
            act1(vb, src, AF.Copy, self.src_key_fn(m), [tk[ti]])
            act1(vs, src, AF.Square, self.src_key_fn(m), [tk[ti]])
            first = self.count == 0
            last = self.count == self.nch - 1
            self.count += 1
            self.flush()

            def stat(first=first, last=last, vb=vb, vs=vs, ti=ti):
                def fn(e):
                    e.matmul(psP(2), onesb[:, :], vb[:, 0:TP], start=first, stop=last)
                    e.matmul(psS(2), onesb[:, :], vb[:, TP:T], start=first, stop=last)
                    e.matmul(psP(3), onesb[:, :], vs[:, 0:TP], start=first, stop=last)
                    return e.matmul(psS(3), onesb[:, :], vs[:, TP:T], start=first, stop=last)
                S.add("pe", fn, r=[tk[ti], "onesb"], w=["ps2", "ps3"])
            self.pending = stat

        def flush(self):
            if self.pending is not None:
                self.pending()
                self.pending = None

        def finalize(self, nchan):
            self.flush()
            inv = 1.0 / nchan
            mean, rstd, var = tf(5), tf(6), tf(7)

            def f1(e):
                e.tensor_scalar(out=mean[:, 0:TP], in0=psP(2), scalar1=inv, scalar2=None, op0=ALU.mult)
                return e.tensor_scalar(out=mean[:, TP:T], in0=psS(2), scalar1=inv, scalar2=None,
                                       op0=ALU.mult)
            dve(f1, ["ps2"], [tk[5]])
            dve(lambda e: e.tensor_tensor(out=rstd, in0=mean, in1=mean, op=ALU.mult), [tk[5]], [tk[6]])

            def f2(e):
                e.scalar_tensor_tensor(out=var[:, 0:TP], in0=psP(3), scalar=inv, in1=rstd[:, 0:TP],
                                       op0=ALU.mult, op1=ALU.subtract)
                return e.scalar_tensor_tensor(out=var[:, TP:T], in0=psS(3), scalar=inv,
                                              in1=rstd[:, TP:T], op0=ALU.mult, op1=ALU.subtract)
            dve(f2, ["ps3", tk[6]], [tk[7]])
            dve(lambda e: e.tensor_scalar(out=var, in0=var, scalar1=EPS, scalar2=None, op0=ALU.add), [tk[7]], [tk[7]])
            act1(var, var, AF.Sqrt, [tk[7]], [tk[7]])
            dve(lambda e: e.reciprocal(out=rstd, in_=var), [tk[7]], [tk[6]])
            return mean, rstd

    def tile_layer(q, l, last_layer):
        S.epoch += 1
        seq0 = NS * q
        tok0 = TP * q

        if l == 0:
            S.add("sp", lambda e: e.dma_start(out=xres[:, :, 0:TP], in_=xpT[:, :, tok0:tok0 + TP]),
                  w=[f"xres{m}" for m in range(16)], slot="xin_p")
            S.add("sp", lambda e: e.dma_start(out=xres[:, :, TP:T], in_=xsT[:, :, TS * q:TS * (q + 1)]),
                  w=[f"xres{m}" for m in range(16)], slot="xin_s")
            for g in range(4):
                gs = slice(4 * g, 4 * g + 4)
                act1(xb[:, gs, :], xres[:, gs, :], AF.Copy, [f"xres{m}" for m in range(4 * g, 4 * g + 4)],
                     [f"xb{m}" for m in range(4 * g, 4 * g + 4)])
        S.add("pool", lambda e: e.dma_start(out=pb[:, :, 0:TP], in_=ppT[l, :, :, tok0:tok0 + TP]), w=["pb"],
              slot="pin_p")
        S.add("pool", lambda e: e.dma_start(out=pb[:, :, TP:T], in_=psT[l, :, :, TS * q:TS * (q + 1)]), w=["pb"],
              slot="pin_s")
        S.add("sp", lambda e: e.dma_start(out=stg_a[:, :, :, :], in_=scaT[l, :, :, seq0:seq0 + NS, :]), w=["stg_a"],
              slot="stg_a")
        S.add("sp", lambda e: e.dma_start(out=stg_b[:, :, :, :], in_=scbT[l, :, :, seq0:seq0 + NS, :]), w=["stg_b"],
              slot="stg_b")
        S.add("sp", lambda e: e.dma_start(out=stg_h[:, :, :], in_=srgT[l, :, :, seq0:seq0 + NS]), w=["stg_h"],
              slot="stg_h")
        S.add("sp", lambda e: e.dma_start(out=stg_f[:, :, :, :], in_=scfT[l, :, :, seq0:seq0 + NS, :]), w=["stg_f"],
              slot="stg_f")
        S.add("sp", lambda e: e.dma_start(out=oas[l, :, :, seq0:seq0 + NS, 0:22],
                                          in_=scaT[l, :, :, seq0:seq0 + NS, 8:30]), slot="o_as0")

        xbk = [f"xb{k}" for k in range(16)]

        def xb_rhs(k):
            return xb[:, k, :]

        ps_allowed[0] = [0, 1]
        lna = LN(8, lambda m: ca32[:, m, slice(0, T)], cak)
        for c in range(8):
            def fdg(e, c=c):
                ins = None
                for k in range(31):
                    ins = e.tensor_scalar(out=dg[:, k, :], in0=identb[:, :], scalar1=pcol(l, "caw", c * 31 + k),
                                          scalar2=None, op0=ALU.mult)
                return ins
            dve(fdg, ["identb", "prm"], ["dg"])
            sv, kv = load_w(W_in[l, c], 2048)
            sg_, kg = load_w(W_in[l, 8 + c], 2048)
            pg = next_ps()
            mm(pg, lambda k, s=sg_: ring[s][:, k * 128:(k + 1) * 128], xb_rhs, 16, [kg] + xbk)
            pv = next_ps()
            mm(pv, lambda k, s=sv: ring[s][:, k * 128:(k + 1) * 128], xb_rhs, 16, [kv] + xbk)
            sg = tf(0)
            act2(pg, sg[:, 0:TP], sg[:, TP:T], AF.Sigmoid, [tk[0]])
            u32 = tf(1)

            def fu(e, pv=pv, sg=sg, u32=u32):
                e.tensor_tensor(out=u32[:, 0:TP], in0=psP(pv), in1=sg[:, 0:TP], op=ALU.mult)
                return e.tensor_tensor(out=u32[:, TP:T], in0=psS(pv), in1=sg[:, TP:T], op=ALU.mult)
            dve(fu, [f"ps{pv}", tk[0]], [tk[1]])
            hp = histp[c % 2]
            hs_ = hists[c % 2]
            hk = f"hist{c % 2}"
            act1(hp[:, 0:30], car_a[:, l, c, :], AF.Copy, ["car_a"], [hk])
            act1(hs_[:, :, 0:30], stg_a[:, c, :, :], AF.Copy, ["stg_a"], [hk])
            act1(hp[:, 30:30 + TP], u32[:, 0:TP], AF.Copy, [tk[1]], [hk])
            act1(hs_[:, :, 30:38], s4(u32[:, TP:T]), AF.Copy, [tk[1]], [hk])
            dve(lambda e, c=c, u32=u32: e.tensor_copy(out=car_a[:, l, c, :], in_=u32[:, TP - 30:TP]), [tk[1], hk],
                ["car_a"])
            dve(lambda e, c=c, u32=u32: e.tensor_copy(out=os_a[:, c, :, :], in_=s4(u32[:, TP:T])), [tk[1]],
                ["os_a"])
            pc = next_ps()

            def fconv(e, hp=hp, hs_=hs_, pc=pc):
                ins = None
                for k in range(31):
                    e.matmul(psP(pc), dg[:, k, :], hp[:, k:k + TP], start=(k == 0), stop=(k == 30))
                for k in range(31):
                    ins = e.matmul(psS(pc).rearrange("p (s j) -> p s j", j=8), dg[:, k, :],
                                   hs_[:, :, k:k + 8], start=(k == 0), stop=(k == 30))
                return ins
            S.add("pe", fconv, r=["dg", hk], w=[f"ps{pc}"])
            act2(pc, ca32[:, c, slice(0, TP)], ca32[:, c, slice(TP, T)], AF.Identity, cak(c), ["prm"],
                 bias=pcol(l, "cab", c), scale=1.0)
            lna.feed(c)
        mean, rstd = lna.finalize(DC)
        for c in range(8):
            ti = (2, 0, 1)[c % 3]
            tn = tf(ti)
            dve(lambda e, c=c, tn=tn: e.tensor_tensor(out=tn, in0=ca32[:, c, slice(0, T)], in1=mean,
                                                       op=ALU.subtract), cak(c) + [tk[5]], [tk[ti]])
            dve(lambda e, tn=tn: e.tensor_tensor(out=tn, in0=tn, in1=rstd, op=ALU.mult), [tk[ti], tk[6]], [tk[ti]])
            act1(work[:, c, :], tn, AF.Silu, [tk[ti], "prm"], [f"work{c}"], scale=pcol(l, "lag", c),
                 bias=pcol(l, "lab", c))

        ps_allowed[0] = [0, 1, 2, 3]
        sAk = [f"work{k}" for k in range(8)]
        for m in range(16):
            s1, k1 = load_w(W_in[l, 48 + m], 2048)
            s2, k2 = load_w(W_aout[l, m], 1024)
            p1 = next_ps()
            mm(p1, lambda k, s=s1: ring[s][:, k * 128:(k + 1) * 128], xb_rhs, 16, [k1] + xbk)
            p2 = next_ps()
            mm(p2, lambda k, s=s2: ring[s][:, k * 128:(k + 1) * 128], lambda k: work[:, k, :], 8, [k2] + sAk)
            sg = tf(m % 2)
            act2(p1, sg[:, 0:TP], sg[:, TP:T], AF.Sigmoid, [tk[m % 2]])

            def fm(e, p2=p2, sg=sg, m=m):
                e.tensor_tensor(out=work[:, 24 + m, 0:TP], in0=psP(p2), in1=sg[:, 0:TP], op=ALU.mult)
                return e.tensor_tensor(out=work[:, 24 + m, TP:T], in0=psS(p2), in1=sg[:, TP:T],
                                       op=ALU.mult)
            dve(fm, [f"ps{p2}", tk[m % 2]], [f"work{24 + m}"])

        sgt, kgt = None, None
        S.add("pool", lambda e: e.dma_start(out=wgate[:, 0, :], in_=W_gate[l, 0], max_dma_last_dim=8192),
              w=["wgate"], slot="wgate")
        S.add("pool", lambda e: e.dma_start(out=wgate[:, 1, :], in_=W_gate[l, 1], max_dma_last_dim=8192),
              w=["wgate"], slot="wgate")
        st = {}

        def m2_pe_proj(c):
            sx, kx = load_w(W_in[l, 16 + c], 2048)
            sgl, kgl = load_w(W_in[l, 32 + c], 2048)
            px = next_ps()
            mm(px, lambda k, s=sx: ring[s][:, k * 128:(k + 1) * 128], xb_rhs, 16, [kx] + xbk)
            pgt = next_ps()
            mm(pgt, lambda k, s=sgl: ring[s][:, k * 128:(k + 1) * 128], xb_rhs, 16, [kgl] + xbk)
            st[c] = (px, pgt)

        def m2_rest_a(c):
            px, pgt = st[c]
            hi = c % 2
            hb = tmp[hi]
            hbs = tmp[hi][:, 520:520 + NS * 11].rearrange("p (s j) -> p s j", j=11)
            dve(lambda e: e.tensor_copy(out=hb[:, 0:3], in_=car_b[:, l, c, :]), ["car_b"], [tk[hi]])
            dve(lambda e: e.tensor_copy(out=hbs[:, :, 0:3], in_=stg_b[:, c, :, :]), ["stg_b"], [tk[hi]])

            def fev(e):
                e.tensor_copy(out=hb[:, 3:3 + TP], in_=psP(px))
                return e.tensor_copy(out=hbs[:, :, 3:11], in_=psS(px).rearrange("p (s j) -> p s j", j=8))
            dve(fev, [f"ps{px}"], [tk[hi]])
            act2(pgt, work[:, 8 + c, 0:TP], work[:, 8 + c, TP:T], AF.Gelu_apprx_tanh, [f"work{8 + c}"])
            dve(lambda e: e.tensor_copy(out=car_b[:, l, c, :], in_=hb[:, TP:TP + 3]), [tk[hi]], ["car_b"])
            dve(lambda e: e.tensor_copy(out=os_b[:, c, :, :], in_=hbs[:, :, 8:11]), [tk[hi]], ["os_b"])
            hbb = tb(2, 564, 568 * hi)
            dve(lambda e: e.tensor_copy(out=hbb, in_=hb[:, 0:564]), [tk[hi]], [f"hbb{hi}"])

            def fdgb(e):
                ins = None
                for k in range(4):
                    ins = e.tensor_scalar(out=dgb[hi][:, k, :], in0=identb[:, :], scalar1=pcol(l, "cbw", c * 4 + k),
                                          scalar2=None, op0=ALU.mult)
                return ins
            dve(fdgb, ["identb", "prm"], [f"dgb{hi}"])

        def m2_b(c):
            hi = c % 2
            tB = [3, 4, 5, 6, 7] if hi == 0 else [8, 9, 10, 11, 12]
            hbb = tb(2, 564, 568 * hi)
            hbbs = hbb[:, 520:520 + NS * 11].rearrange("p (s j) -> p s j", j=11)
            pcv = next_ps()

            def fconv4(e):
                ins = None
                for k in range(4):
                    e.matmul(psP(pcv), dgb[hi][:, k, :], hbb[:, k:k + TP], start=(k == 0), stop=(k == 3))
                for k in range(4):
                    ins = e.matmul(psS(pcv).rearrange("p (s j) -> p s j", j=8), dgb[hi][:, k, :],
                                   hbbs[:, :, k:k + 8], start=(k == 0), stop=(k == 3))
                return ins
            S.add("pe", fconv4, r=[f"hbb{hi}", f"dgb{hi}"], w=[f"ps{pcv}"])
            cbb = tb(hi)
            act2(pcv, cbb[:, 0:TP], cbb[:, TP:T], AF.Identity, [tk[hi]], ["prm"], bias=pcol(l, "cbb", c), scale=1.0)
            cb = tf(tB[4])

            act2(pcv, cb[:, 0:TP], cb[:, TP:T], AF.Identity, [tk[tB[4]]], ["prm"], bias=pcol(l, "cbb", c), scale=1.0)
            pr = next_ps()
            mm(pr, lambda k: wgate[:, 0, c * 128:(c + 1) * 128], lambda k: cbb, 1, ["wgate", tk[hi]])
            pi = next_ps()
            mm(pi, lambda k: wgate[:, 1, c * 128:(c + 1) * 128], lambda k: cbb, 1, ["wgate", tk[hi]])
            r_ = tf(tB[0])
            gi = tf(tB[1])
            act2(pr, r_[:, 0:TP], r_[:, TP:T], AF.Sigmoid, [tk[tB[0]]], ["prm"], bias=pcol(l, "br", c), scale=1.0)
            act2(pi, gi[:, 0:TP], gi[:, TP:T], AF.Sigmoid, [tk[tB[1]]], ["prm"], bias=pcol(l, "bi", c), scale=1.0)
            act1(r_, r_, AF.Exp, [tk[tB[0]], "nsp8"], [tk[tB[0]]], scale=nsp8[:, l, c:c + 1])
            om = tf(tB[2])
            dve(lambda e: e.tensor_tensor(out=om, in0=r_, in1=r_, op=ALU.mult), [tk[tB[0]]], [tk[tB[2]]])
            dve(lambda e: e.tensor_scalar(out=om, in0=om, scalar1=-1.0, scalar2=1.0, op0=ALU.mult, op1=ALU.add),
                [tk[tB[2]]], [tk[tB[2]]])
            act1(om, om, AF.Sqrt, [tk[tB[2]]], [tk[tB[2]]])
            if q == 0:
                dve(lambda e: e.memset(om[:, 0:1], 1.0), [tk[tB[2]]], [tk[tB[2]]])

            dve(lambda e: e.tensor_tensor(out=gi, in0=gi, in1=cb, op=ALU.mult), [tk[tB[1]], tk[tB[4]]], [tk[tB[1]]])
            dve(lambda e: e.tensor_tensor(out=om, in0=gi, in1=om, op=ALU.mult), [tk[tB[1]], tk[tB[2]]], [tk[tB[2]]])
            hs = tf(tB[3])

            def fscan(e):
                e.tensor_tensor_scan(out=hs[:, 0:TP], data0=r_[:, 0:TP], data1=om[:, 0:TP],
                                     initial=car_h[:, l, c:c + 1], op0=ALU.mult, op1=ALU.add)
                ins = None
                for s_ in range(NS):
                    a0 = TP + 8 * s_
                    ins = e.tensor_tensor_scan(out=hs[:, a0:a0 + 8], data0=r_[:, a0:a0 + 8], data1=om[:, a0:a0 + 8],
                                               initial=stg_h[:, c, s_:s_ + 1], op0=ALU.mult, op1=ALU.add)
                return ins
            dve(fscan, [tk[tB[0]], tk[tB[2]], "car_h", "stg_h"], [tk[tB[3]]])
            dve(lambda e: e.tensor_copy(out=car_h[:, l, c:c + 1], in_=hs[:, TP - 1:TP]), [tk[tB[3]]], ["car_h"])
            dve(lambda e: e.tensor_copy(out=os_h[:, c, :], in_=s4(hs[:, TP:T])[:, :, 7]), [tk[tB[3]]], ["os_h"])
            dve(lambda e: e.tensor_tensor(out=work[:, 8 + c, :], in0=hs, in1=work[:, 8 + c, :], op=ALU.mult),
                [tk[tB[3]], f"work{8 + c}"], [f"work{8 + c}"])

        m2_pe_proj(0)
        m2_rest_a(0)
        for c in range(16):
            if c + 1 < 16:
                m2_pe_proj(c + 1)
                m2_rest_a(c + 1)
            m2_b(c)

        hgk = [f"work{8 + k}" for k in range(16)]
        for m in range(16):
            s1, k1 = load_w(W_in[l, 64 + m], 2048)
            s2, k2 = load_w(W_bout[l, m], 2048)
            p1 = next_ps()
            mm(p1, lambda k, s=s1: ring[s][:, k * 128:(k + 1) * 128], xb_rhs, 16, [k1] + xbk)
            p2 = next_ps()
            mm(p2, lambda k, s=s2: ring[s][:, k * 128:(k + 1) * 128], lambda k: work[:, 8 + k, :], 16, [k2] + hgk)
            sg = tf(m % 2)
            act2(p1, sg[:, 0:TP], sg[:, TP:T], AF.Sigmoid, [tk[m % 2]])
            t2 = tf(2 + m % 2)

            def fm(e, p2=p2, sg=sg, t2=t2):
                e.tensor_tensor(out=t2[:, 0:TP], in0=psP(p2), in1=sg[:, 0:TP], op=ALU.mult)
                return e.tensor_tensor(out=t2[:, TP:T], in0=psS(p2), in1=sg[:, TP:T], op=ALU.mult)
            dve(fm, [f"ps{p2}", tk[m % 2]], [tk[2 + m % 2]])
            dve(lambda e, m=m, t2=t2: e.tensor_tensor(out=work[:, 24 + m, :], in0=work[:, 24 + m, :], in1=t2,
                                                       op=ALU.add), [tk[2 + m % 2], f"work{24 + m}"],
                [f"work{24 + m}"])

        def residual_ln(wsrc, nk, rhs_fn, rkeys, gname, bname, extra=None):
            ps_allowed[0] = [0, 1]
            ln = LN(16, lambda m: xres[:, m, :], lambda m: [f"xres{m}"])
            for m in range(16):
                if extra is None:
                    s1, k1 = load_w(wsrc(m), nk * 128)
                    p1 = next_ps()
                    mm(p1, lambda k, s=s1: ring[s][:, k * 128:(k + 1) * 128], rhs_fn, nk, [k1] + rkeys)
                else:
                    p1 = extra(m)

                def fr(e, p1=p1, m=m):
                    e.scalar_tensor_tensor(out=xres[:, m, 0:TP], in0=xres[:, m, 0:TP], scalar=ALPHA,
                                           in1=psP(p1), op0=ALU.mult, op1=ALU.add)
                    return e.scalar_tensor_tensor(out=xres[:, m, TP:T], in0=xres[:, m, TP:T], scalar=ALPHA,
                                                  in1=psS(p1), op0=ALU.mult, op1=ALU.add)
                dve(fr, [f"ps{p1}", f"xres{m}"], [f"xres{m}"])
                ln.feed(m)
            mean, rstd = ln.finalize(D)
            for m in range(16):
                ti = (2, 0, 1)[m % 3]
                tn = tf(ti)
                dve(lambda e, m=m, tn=tn: e.tensor_tensor(out=tn, in0=xres[:, m, :], in1=mean, op=ALU.subtract),
                    [f"xres{m}", tk[5]], [tk[ti]])
                dve(lambda e, tn=tn: e.tensor_tensor(out=tn, in0=tn, in1=rstd, op=ALU.mult), [tk[ti], tk[6]],
                    [tk[ti]])
                act1(xres[:, m, :], tn, AF.Identity, [tk[ti], "prm"], [f"xres{m}"], scale=pcol(l, gname, m),
                     bias=pcol(l, bname, m))
                act1(xb[:, m, :], xres[:, m, :], AF.Copy, [f"xres{m}"], [f"xb{m}"])
            ps_allowed[0] = [0, 1, 2, 3]

        mgk = [f"work{24 + k}" for k in range(16)]
        residual_ln(lambda m: W_o[l, m], 16, lambda k: work[:, 24 + k, :], mgk, "l1g", "l1b")

        ffn_pre = {}
        gl = []
        for j in (0, 1):
            sgw, kgw = load_w(W_up[l, 48 + j], 2048)
            suw, kuw = load_w(W_up[l, j], 2048)
            pgp = next_ps()
            pup = next_ps()
            gl.append((pgp, lambda k, s=sgw: ring[s][:, k * 128:(k + 1) * 128], kgw))
            gl.append((pup, lambda k, s=suw: ring[s][:, k * 128:(k + 1) * 128], kuw))
            ffn_pre[j] = (pgp, pup)
        mm_kouter(gl, xb_rhs, 16, xbk)
        for j in range(48):
            if j in ffn_pre:
                pgp, pup = ffn_pre[j]
            else:
                sgw, kgw = load_w(W_up[l, 48 + j], 2048)
                suw, kuw = load_w(W_up[l, j], 2048)
                pgp = next_ps()
                mm(pgp, lambda k, s=sgw: ring[s][:, k * 128:(k + 1) * 128], xb_rhs, 16, [kgw] + xbk)
                pup = next_ps()
                mm(pup, lambda k, s=suw: ring[s][:, k * 128:(k + 1) * 128], xb_rhs, 16, [kuw] + xbk)
            hi = j % 2
            hf = tmp[hi]
            hfs = tmp[hi][:, 520:520 + NS * 10].rearrange("p (s j) -> p s j", j=10)
            dve(lambda e, j=j, hf=hf: e.tensor_copy(out=hf[:, 0:2], in_=car_f[:, l, j, :]), ["car_f"], [tk[hi]])
            dve(lambda e, j=j, hfs=hfs: e.tensor_copy(out=hfs[:, :, 0:2], in_=stg_f[:, j, :, :]), ["stg_f"], [tk[hi]])
            act2(pgp, hf[:, 2:2 + TP], hfs[:, :, 2:10], AF.Identity, [tk[hi]], s3=True)
            dve(lambda e, j=j, hf=hf: e.tensor_copy(out=car_f[:, l, j, :], in_=hf[:, TP:TP + 2]), [tk[hi]], ["car_f"])
            dve(lambda e, j=j, hfs=hfs: e.tensor_copy(out=os_f[:, j, :, :], in_=hfs[:, :, 8:10]), [tk[hi]], ["os_f"])
            fc = tf(2 + hi)

            def ffc(e, j=j, hf=hf, hfs=hfs, fc=fc):
                fcs = s4(fc[:, TP:T])
                e.tensor_scalar(out=fc[:, 0:TP], in0=hf[:, 0:TP], scalar1=pcol(l, "fcw", j * 3),
                                scalar2=pcol(l, "fcb", j), op0=ALU.mult, op1=ALU.add)
                e.tensor_scalar(out=fcs, in0=hfs[:, :, 0:8], scalar1=pcol(l, "fcw", j * 3),
                                scalar2=pcol(l, "fcb", j), op0=ALU.mult, op1=ALU.add)
                ins = None
                for k in range(1, 3):
                    e.scalar_tensor_tensor(out=fc[:, 0:TP], in0=hf[:, k:k + TP], scalar=pcol(l, "fcw", j * 3 + k),
                                           in1=fc[:, 0:TP], op0=ALU.mult, op1=ALU.add)
                    ins = e.scalar_tensor_tensor(out=fcs, in0=hfs[:, :, k:k + 8], scalar=pcol(l, "fcw", j * 3 + k),
                                                 in1=fcs, op0=ALU.mult, op1=ALU.add)
                return ins
            dve(ffc, [tk[hi], "prm"], [tk[2 + hi]])
            act1(fc, fc, AF.Gelu_apprx_tanh, [tk[2 + hi]], [tk[2 + hi]])

            def fh(e, j=j, pup=pup, fc=fc):
                e.tensor_tensor(out=work[:, j, 0:TP], in0=psP(pup), in1=fc[:, 0:TP], op=ALU.mult)
                return e.tensor_tensor(out=work[:, j, TP:T], in0=psS(pup), in1=fc[:, TP:T], op=ALU.mult)
            dve(fh, [f"ps{pup}", tk[2 + hi]], [f"work{j}"])

        hk_all = [f"work{k}" for k in range(48)]

        def down_extra(m):
            sa_, ka = load_w(W_down[l, 2 * m], 24 * 128)
            sb_, kb = load_w(W_down[l, 2 * m + 1], 24 * 128)
            p1 = next_ps()
            mm(p1, lambda k, s=sa_: ring[s][:, k * 128:(k + 1) * 128], lambda k: work[:, k, :], 24, [ka] + hk_all[:24],
               first=True, last=False)
            mm(p1, lambda k, s=sb_: ring[s][:, k * 128:(k + 1) * 128], lambda k: work[:, 24 + k, :], 24,
               [kb] + hk_all[24:], first=False, last=True)
            return p1
        residual_ln(None, 0, None, None, "l2g", "l2b", extra=down_extra)

        def ple_extra(m):
            s1, k1 = load_w(W_pg[l, m], 2048)
            s2, k2 = load_w(W_pe[l, m], 256)
            p1 = next_ps()
            mm(p1, lambda k, s=s1: ring[s][:, k * 128:(k + 1) * 128], xb_rhs, 16, [k1] + xbk)
            sg = tf(m % 2)
            act2(p1, sg[:, 0:TP], sg[:, TP:T], AF.Sigmoid, [tk[m % 2]], ["prm"], bias=pcol(l, "bpg", m), scale=1.0)
            p2 = next_ps()
            mm(p2, lambda k, s=s2: ring[s][:, k * 128:(k + 1) * 128], lambda k: pb[:, k, :], 2, [k2, "pb"])
            def fe(e, p2=p2, sg=sg):
                e.tensor_tensor(out=sg[:, 0:TP], in0=psP(p2), in1=sg[:, 0:TP], op=ALU.mult)
                return e.tensor_tensor(out=sg[:, TP:T], in0=psS(p2), in1=sg[:, TP:T], op=ALU.mult)
            dve(fe, [f"ps{p2}", tk[m % 2]], [tk[m % 2]])
            return ("sb", m % 2)
        ps_allowed[0] = [0, 1]
        ln = LN(16, lambda m: xres[:, m, :], lambda m: [f"xres{m}"])
        for m in range(16):
            _, ti = ple_extra(m)
            dve(lambda e, m=m, ti=ti: e.scalar_tensor_tensor(out=xres[:, m, :], in0=xres[:, m, :], scalar=ALPHA,
                                                              in1=tf(ti), op0=ALU.mult, op1=ALU.add),
                [tk[ti], f"xres{m}"], [f"xres{m}"])
            ln.feed(m)
        mean, rstd = ln.finalize(D)
        for m in range(16):
            ti = (2, 0, 1)[m % 3]
            tn = tf(ti)
            dve(lambda e, m=m, tn=tn: e.tensor_tensor(out=tn, in0=xres[:, m, :], in1=mean, op=ALU.subtract),
                [f"xres{m}", tk[5]], [tk[ti]])
            dve(lambda e, tn=tn: e.tensor_tensor(out=tn, in0=tn, in1=rstd, op=ALU.mult), [tk[ti], tk[6]], [tk[ti]])
            act1(xres[:, m, :], tn, AF.Identity, [tk[ti], "prm"], [f"xres{m}"], scale=pcol(l, "l3g", m),
                 bias=pcol(l, "l3b", m))
            if not last_layer:
                act1(xb[:, m, :], xres[:, m, :], AF.Copy, [f"xres{m}"], [f"xb{m}"])
        ps_allowed[0] = [0, 1, 2, 3]

        xk = [f"xres{m}" for m in range(16)]
        if last_layer:
            S.add("sp", lambda e: e.dma_start(out=ypT[:, :, tok0:tok0 + TP], in_=xres[:, :, 0:TP]), r=xk, slot="yo_p")
            S.add("sp", lambda e: e.dma_start(out=ysT[:, :, TS * q:TS * (q + 1)], in_=xres[:, :, TP:T]), r=xk,
                  slot="yo_s")
        S.add("sp", lambda e: e.dma_start(out=oas[l, :, :, seq0:seq0 + NS, 22:30], in_=os_a[:, :, :, :]), r=["os_a"],
              slot="o_as")
        S.add("sp", lambda e: e.dma_start(out=obs[l, :, :, seq0:seq0 + NS, :], in_=os_b[:, :, :, :]), r=["os_b"],
              slot="o_bs")
        S.add("sp", lambda e: e.dma_start(out=ohs[l, :, :, seq0:seq0 + NS], in_=os_h[:, :, :]), r=["os_h"],
              slot="o_hs")
        S.add("sp", lambda e: e.dma_start(out=ofs[l, :, :, seq0:seq0 + NS, :], in_=os_f[:, :, :, :]), r=["os_f"],
              slot="o_fs")
        if q == ntile - 1:
            S.add("sp", lambda e: e.dma_start(out=oap[l, :, :, :], in_=car_a[:, l, :, :]), r=["car_a"], slot="o_ap")
            S.add("sp", lambda e: e.dma_start(out=obp[l, :, :, :], in_=car_b[:, l, :, :]), r=["car_b"], slot="o_bp")
            S.add("sp", lambda e: e.dma_start(out=ohp[l, :, :], in_=car_h[:, l, :]), r=["car_h"], slot="o_hp")
            S.add("sp", lambda e: e.dma_start(out=ofp[l, :, :, :], in_=car_f[:, l, :, :]), r=["car_f"], slot="o_fp")

    for q in range(ntile):
        for l in range(nlayer):
            tile_layer(q, l, l == nlayer - 1)

    final = [s for s in S.slot_cnt if s.startswith("o_") or s.startswith("yo_")]
    S.emit(nc, stack, final)
    stack.close()
    return nc


def _panels(w, kc):
    K, N = w.shape
    assert K == kc * 128
    return np.ascontiguousarray(w.reshape(kc, 128, N // 128, 128).transpose(2, 1, 0, 3)).reshape(N // 128, 128,
                                                                                                 kc * 128)


def _vec(v):
    return v.reshape(-1, 128).T


_NC_CACHE = {}
_PREP_ONLY = [False]


def kernel(x_prompt, x_sample, state_conv_a, state_conv_b, state_rglru, state_conv_ffn, p_prompt, p_sample,
           w_in, conv_a_w, conv_a_b, ln_a_g, ln_a_b, w_a_out, conv_b_w, conv_b_b, w_r, b_r, w_i, b_i,
           lru_lambda, w_b_out, w_o, ln1_g, ln1_b, w_up, ffn_conv_w, ffn_conv_b, w_down, ln2_g, ln2_b,
           w_pe, w_pg, b_pg, ln3_g, ln3_b):
    f = lambda a: np.asarray(a, dtype=np.float32)
    (x_prompt, x_sample, state_conv_a, state_conv_b, state_rglru, state_conv_ffn, p_prompt, p_sample, w_in,
     conv_a_w, conv_a_b, ln_a_g, ln_a_b, w_a_out, conv_b_w, conv_b_b, w_r, b_r, w_i, b_i, lru_lambda, w_b_out, w_o,
     ln1_g, ln1_b, w_up, ffn_conv_w, ffn_conv_b, w_down, ln2_g, ln2_b, w_pe, w_pg, b_pg, ln3_g, ln3_b) = map(f, (
         x_prompt, x_sample, state_conv_a, state_conv_b, state_rglru, state_conv_ffn, p_prompt, p_sample, w_in,
         conv_a_w, conv_a_b, ln_a_g, ln_a_b, w_a_out, conv_b_w, conv_b_b, w_r, b_r, w_i, b_i, lru_lambda, w_b_out,
         w_o, ln1_g, ln1_b, w_up, ffn_conv_w, ffn_conv_b, w_down, ln2_g, ln2_b, w_pe, w_pg, b_pg, ln3_g, ln3_b))

    shared = {
        "W_in": np.stack([_panels(w_in[l], 16) for l in range(L)]),
        "W_aout": np.stack([_panels(w_a_out[l], 8) for l in range(L)]),
        "W_bout": np.stack([_panels(w_b_out[l], 16) for l in range(L)]),
        "W_o": np.stack([_panels(w_o[l], 16) for l in range(L)]),
        "W_up": np.stack([_panels(w_up[l], 16) for l in range(L)]),
        "W_pg": np.stack([_panels(w_pg[l], 16) for l in range(L)]),
        "W_pe": np.stack([_panels(w_pe[l], 2) for l in range(L)]),
        "ident": np.eye(128, dtype=np.float32),
    }
    wd = []
    for l in range(L):
        a = _panels(w_down[l][:3072], 24)
        b = _panels(w_down[l][3072:], 24)
        wd.append(np.stack([a, b], axis=1).reshape(32, 128, 24 * 128))
    shared["W_down"] = np.stack(wd)
    wg = []
    for l in range(L):
        wg.append(np.stack([np.ascontiguousarray(w_r[l].transpose(1, 0, 2)).reshape(128, 2048),
                            np.ascontiguousarray(w_i[l].transpose(1, 0, 2)).reshape(128, 2048)]))
    shared["W_gate"] = np.stack(wg)
    prm = np.zeros((L, 128, NV), np.float32)
    for l in range(L):
        P = prm[l]
        P[:, _off["caw"]:_off["caw"] + 248] = conv_a_w[l].reshape(31, 8, 128).transpose(2, 1, 0).reshape(128, 248)
        P[:, _off["cab"]:_off["cab"] + 8] = _vec(conv_a_b[l])
        P[:, _off["lag"]:_off["lag"] + 8] = _vec(ln_a_g[l])
        P[:, _off["lab"]:_off["lab"] + 8] = _vec(ln_a_b[l])
        P[:, _off["cbw"]:_off["cbw"] + 64] = conv_b_w[l].reshape(4, 16, 128).transpose(2, 1, 0).reshape(128, 64)
        P[:, _off["cbb"]:_off["cbb"] + 16] = _vec(conv_b_b[l])
        P[:, _off["br"]:_off["br"] + 16] = _vec(b_r[l])
        P[:, _off["bi"]:_off["bi"] + 16] = _vec(b_i[l])
        P[:, _off["lam"]:_off["lam"] + 16] = _vec(lru_lambda[l])
        P[:, _off["l1g"]:_off["l1g"] + 16] = _vec(ln1_g[l])
        P[:, _off["l1b"]:_off["l1b"] + 16] = _vec(ln1_b[l])
        P[:, _off["fcw"]:_off["fcw"] + 144] = ffn_conv_w[l].reshape(3, 48, 128).transpose(2, 1, 0).reshape(128, 144)
        P[:, _off["fcb"]:_off["fcb"] + 48] = _vec(ffn_conv_b[l])
        P[:, _off["l2g"]:_off["l2g"] + 16] = _vec(ln2_g[l])
        P[:, _off["l2b"]:_off["l2b"] + 16] = _vec(ln2_b[l])
        P[:, _off["bpg"]:_off["bpg"] + 16] = _vec(b_pg[l])
        P[:, _off["l3g"]:_off["l3g"] + 16] = _vec(ln3_g[l])
        P[:, _off["l3b"]:_off["l3b"] + 16] = _vec(ln3_b[l])
    shared["prm"] = prm

    def cm(a, nch):
        rows = a.reshape(-1, nch, 128)
        return np.ascontiguousarray(rows.transpose(2, 1, 0))

    in_maps = []
    zx = np.zeros((128, 16, 2048), np.float32)
    zp = np.zeros((L, 128, 2, 2048), np.float32)
    for c in range(NCORES):
        sq = slice(16 * c, 16 * c + 16)
        m = dict(shared)
        if c in PROMPT_CORES:
            s = PROMPT_CORES.index(c)
            m["xpT"] = cm(x_prompt[s], 16)
            m["ppT"] = np.stack([cm(p_prompt[l, s], 2) for l in range(L)])
        else:
            m["xpT"] = zx
            m["ppT"] = zp
        m["xsT"] = cm(x_sample[sq], 16)
        m["psT"] = np.stack([cm(p_sample[l, sq], 2) for l in range(L)])
        m["scaT"] = np.stack([cm(state_conv_a[l, sq], 8).reshape(128, 8, 16, 30) for l in range(L)])
        m["scbT"] = np.stack([cm(state_conv_b[l, sq], 16).reshape(128, 16, 16, 3) for l in range(L)])
        m["srgT"] = np.stack([cm(state_rglru[l, sq], 16).reshape(128, 16, 16) for l in range(L)])
        m["scfT"] = np.stack([cm(state_conv_ffn[l, sq], 48).reshape(128, 48, 16, 2) for l in range(L)])
        in_maps.append(m)

    if _PREP_ONLY[0]:
        return in_maps
    if "nc" not in _NC_CACHE:
        _NC_CACHE["nc"] = build_nc()
    nc = _NC_CACHE["nc"]
    res = run_bass_kernel_spmd(nc, in_maps, core_ids=list(range(NCORES)))
    R = res.results
    return _post(R)


def _post(R):

    def tm(a):
        nch = a.shape[1]
        rest = a.shape[2:]
        return np.ascontiguousarray(np.moveaxis(a.reshape(128, nch, -1), 2, 0).transpose(0, 2, 1)).reshape(
            *rest, nch * 128)

    y_prompt = np.stack([tm(R[c]["ypT"]) for c in PROMPT_CORES])
    y_sample = np.concatenate([tm(R[c]["ysT"]).reshape(16, 8, D) for c in range(NCORES)], axis=0)
    na_p = np.stack([np.stack([tm(R[c]["oap"][l]) for c in PROMPT_CORES]) for l in range(L)])
    nb_p = np.stack([np.stack([tm(R[c]["obp"][l]) for c in PROMPT_CORES]) for l in range(L)])
    nh_p = np.stack([np.stack([tm(R[c]["ohp"][l]) for c in PROMPT_CORES]) for l in range(L)])
    nf_p = np.stack([np.stack([tm(R[c]["ofp"][l]) for c in PROMPT_CORES]) for l in range(L)])
    na_s = np.stack([np.concatenate([tm(R[c]["oas"][l]) for c in range(NCORES)], axis=0) for l in range(L)])
    nb_s = np.stack([np.concatenate([tm(R[c]["obs"][l]) for c in range(NCORES)], axis=0) for l in range(L)])
    nh_s = np.stack([np.concatenate([tm(R[c]["ohs"][l]) for c in range(NCORES)], axis=0) for l in range(L)])
    nf_s = np.stack([np.concatenate([tm(R[c]["ofs"][l]) for c in range(NCORES)], axis=0) for l in range(L)])
    outs = (y_prompt, y_sample, na_p, nb_p, nh_p, nf_p, na_s, nb_s, nh_s, nf_s)
    return tuple(np.ascontiguousarray(o, dtype=np.float32) for o in outs)
```

```python
import contextlib
import numpy as np
import concourse.bass as bass
import concourse.mybir as mybir
from concourse.bass_utils import run_bass_kernel_spmd

F32 = mybir.dt.float32
BF16 = mybir.dt.bfloat16
AF = mybir.ActivationFunctionType
ALU = mybir.AluOpType

D = 2048
DC = 1024
DFF = 6144
DPLE = 256
L = 2
TP = 512
NS = 4
TS = NS * 8
T = TP + TS
NTILE = 4
ALPHA = (2.0 * L) ** 0.25
EPS = 1e-5
NCORES = 8
PROMPT_CORES = [0, 1, 4, 5]

_off = {}
_n = 0
for _name, _cnt in [("caw", 8 * 31), ("cab", 8), ("lag", 8), ("lab", 8), ("cbw", 16 * 4), ("cbb", 16), ("br", 16),
                    ("bi", 16), ("lam", 16), ("l1g", 16), ("l1b", 16), ("fcw", 48 * 3), ("fcb", 48), ("l2g", 16),
                    ("l2b", 16), ("bpg", 16), ("l3g", 16), ("l3b", 16)]:
    _off[_name] = _n
    _n += _cnt
NV = _n

ENG = ["pe", "act", "dve", "pool", "sp"]


class Sched:
    def __init__(self):
        self.ops = {e: [] for e in ENG}
        self.last_w = {}
        self.readers = {}
        self.slot_cnt = {}
        self.slot_last = {}
        self.epoch = 0

    def add(self, eng, fn, r=(), w=(), slot=None):
        deps = set()
        for k in tuple(r) + tuple(w):
            lw = self.last_w.get(k)
            if lw is not None:
                deps.add(lw)
        for k in w:
            for rd in self.readers.get(k, ()):
                deps.add(rd)
        if slot is not None and slot in self.slot_last:
            deps.add(self.slot_last[slot])
        idx = len(self.ops[eng])
        if slot is not None:
            n = self.slot_cnt.get(slot, 0) + 1
            self.slot_cnt[slot] = n
            ident = ("dma", slot, n)
            self.slot_last[slot] = ident
        else:
            ident = ("op", eng, idx)
        self.ops[eng].append(dict(fn=fn, deps=deps, epoch=self.epoch, ident=ident, slot=slot, ms=False, msn=0))
        for k in w:
            self.last_w[k] = ident
            self.readers[k] = set()
        for k in r:
            self.readers.setdefault(k, set()).add(ident)
        return ident

    def emit(self, nc, stack, final_slots):
        for e in ENG:
            for op in self.ops[e]:
                for d in op["deps"]:
                    if d[0] == "op":
                        if e == "pe" and d[1] == "pe":
                            continue
                        self.ops[d[1]][d[2]]["ms"] = True
        sems = {}
        for e in ENG:
            cnt = {}
            for op in self.ops[e]:
                if op["ms"] and op["slot"] is None:
                    ep = op["epoch"]
                    cnt[ep] = cnt.get(ep, 0) + 1
                    op["msn"] = cnt[ep]
                    if (e, ep) not in sems:
                        sems[(e, ep)] = stack.enter_context(nc.semaphore(f"s_{e}_{ep}"))
        slot_sems = {s: stack.enter_context(nc.semaphore(f"d_{s}")) for s in self.slot_cnt}
        block = stack.enter_context(nc.Block())
        ops = self.ops
        slot_cnt = self.slot_cnt

        def run(e, eng):
            waited = {}
            for op in ops[e]:
                need = {}
                for d in op["deps"]:
                    if d[0] == "dma":
                        key = ("d", d[1])
                        sem = slot_sems[d[1]]
                        val = 16 * d[2]
                    else:
                        if e == "pe" and d[1] == "pe":
                            continue
                        p = ops[d[1]][d[2]]
                        key = (d[1], p["epoch"])
                        sem = sems[key]
                        val = p["msn"]
                    if key not in need or need[key][1] < val:
                        need[key] = (sem, val)
                for key, (sem, val) in need.items():
                    if waited.get(key, 0) >= val:
                        continue
                    eng.wait_ge(sem, val)
                    waited[key] = val
                ins = op["fn"](eng)
                if op["slot"] is not None:
                    ins.then_inc(slot_sems[op["slot"]], 16)
                elif op["ms"]:
                    ins.then_inc(sems[(e, op["epoch"])], 1)
            if e == "sp":
                for s in final_slots:
                    eng.wait_ge(slot_sems[s], 16 * slot_cnt[s])

        @block.tensor
        def _(t):
            run("pe", t)

        @block.scalar
        def _(t):
            run("act", t)

        @block.vector
        def _(t):
            run("dve", t)

        @block.gpsimd
        def _(t):
            run("pool", t)

        @block.sync
        def _(t):
            run("sp", t)


def build_nc(ntile=NTILE, nlayer=L):
    nc = bass.Bass("TRN2", target_bir_lowering=False)
    S = Sched()
    stack = contextlib.ExitStack()

    def din(name, shape):
        return nc.dram_tensor(name, list(shape), F32, kind="ExternalInput").ap()

    def dout(name, shape):
        return nc.dram_tensor(name, list(shape), F32, kind="ExternalOutput").ap()

    xpT = din("xpT", [128, 16, 2048])
    xsT = din("xsT", [128, 16, 128])
    ppT = din("ppT", [L, 128, 2, 2048])
    psT = din("psT", [L, 128, 2, 128])
    scaT = din("scaT", [L, 128, 8, 16, 30])
    scbT = din("scbT", [L, 128, 16, 16, 3])
    srgT = din("srgT", [L, 128, 16, 16])
    scfT = din("scfT", [L, 128, 48, 16, 2])
    prm = din("prm", [L, 128, NV])
    W_in = din("W_in", [L, 80, 128, 16 * 128])
    W_aout = din("W_aout", [L, 16, 128, 8 * 128])
    W_bout = din("W_bout", [L, 16, 128, 16 * 128])
    W_o = din("W_o", [L, 16, 128, 16 * 128])
    W_up = din("W_up", [L, 96, 128, 16 * 128])
    W_down = din("W_down", [L, 32, 128, 24 * 128])
    W_pg = din("W_pg", [L, 16, 128, 16 * 128])
    W_pe = din("W_pe", [L, 16, 128, 2 * 128])
    W_gate = din("W_gate", [L, 2, 128, 16 * 128])
    identd = din("ident", [128, 128])

    ypT = dout("ypT", [128, 16, 2048])
    ysT = dout("ysT", [128, 16, 128])
    oap = dout("oap", [L, 128, 8, 30])
    obp = dout("obp", [L, 128, 16, 3])
    ohp = dout("ohp", [L, 128, 16])
    ofp = dout("ofp", [L, 128, 48, 2])
    oas = dout("oas", [L, 128, 8, 16, 30])
    obs = dout("obs", [L, 128, 16, 16, 3])
    ohs = dout("ohs", [L, 128, 16, 16])
    ofs = dout("ofs", [L, 128, 48, 16, 2])

    def sb(name, shape, dt=F32):
        return stack.enter_context(nc.sbuf_tensor(name, list(shape), dt))

    xres = sb("xres", [128, 16, T])
    xb = sb("xb", [128, 16, T], BF16)
    work = sb("work", [128, 48, T], BF16)
    histp = [sb(f"histp{i}", [128, 30 + TP], BF16) for i in range(2)]
    hists = [sb(f"hists{i}", [128, NS, 38], BF16) for i in range(2)]
    dg = sb("dg", [128, 31, 128], BF16)
    dgb = [sb(f"dgb{i}", [128, 4, 128], BF16) for i in range(2)]
    NB = 4
    ring = [sb(f"ring{i}", [128, 24 * 128], BF16) for i in range(NB)]
    prm_t = sb("prm_t", [128, L, NV])
    wgate = sb("wgate", [128, 2, 2048], BF16)
    nsp8 = sb("nsp8", [128, L, 16])
    identb = sb("identb", [128, 128], BF16)
    onesb = sb("onesb", [128, 128], BF16)
    car_a = sb("car_a", [128, L, 8, 30])
    car_b = sb("car_b", [128, L, 16, 3])
    car_h = sb("car_h", [128, L, 16])
    car_f = sb("car_f", [128, L, 48, 2])
    stg_a = sb("stg_a", [128, 8, NS, 30])
    stg_b = sb("stg_b", [128, 16, NS, 3])
    stg_h = sb("stg_h", [128, 16, NS])
    stg_f = sb("stg_f", [128, 48, NS, 2])
    os_a = sb("os_a", [128, 8, NS, 8])
    os_b = sb("os_b", [128, 16, NS, 3])
    os_h = sb("os_h", [128, 16, NS])
    os_f = sb("os_f", [128, 48, NS, 2])
    pb = sb("pb", [128, 2, T], BF16)
    NTMP = 13
    TW = 568
    tmp = [sb(f"tmp{i}", [128, TW]) for i in range(NTMP)]
    ca32f = work[:, 24:40, :].rearrange("p a t -> p (a t)").bitcast(F32)

    class _CA:
        def __getitem__(self, idx):
            _, c, sl = idx
            return ca32f[:, c * T:(c + 1) * T][:, sl]
    ca32 = _CA()

    def cak(c):
        return [f"work{24 + 2 * c}", f"work{25 + 2 * c}"]
    psum = [stack.enter_context(nc.psum_tensor(f"ps{i}", [128, 2, 512], F32)) for i in range(4)]

    def psP(i):
        return psum[i][:, 0, 0:TP]

    def psS(i):
        return psum[i][:, 1, 0:TS]

    def tf(i, n=T):
        return tmp[i][:, 0:n]

    def tb(i, n=T, off=0):
        return tmp[i][:, :].bitcast(BF16)[:, off:off + n]

    tk = [f"tmp{i}" for i in range(NTMP)]

    ring_i = [0]
    ps_allowed = [[0, 1, 2, 3]]
    ps_i = [0]

    def next_ps():
        al = ps_allowed[0]
        s = al[ps_i[0] % len(al)]
        ps_i[0] += 1
        return s

    def load_w(src_ap, ncols):
        s = ring_i[0] % NB
        ring_i[0] += 1
        key = f"ring{s}"
        dst = ring[s][:, 0:ncols]
        S.add("pool", lambda e, dst=dst, src_ap=src_ap: e.dma_start(out=dst, in_=src_ap, max_dma_last_dim=8192),
              w=[key], slot=key)
        return s, key

    def mm(ps, lhs_fn, rhs_fn, nk, rkeys, first=True, last=True):
        def fn(e):
            ins = None
            for k in range(nk):
                st = first and k == 0
                sp_ = last and k == nk - 1
                rhs = rhs_fn(k)
                e.matmul(psP(ps), lhs_fn(k), rhs[:, 0:TP], start=st, stop=sp_)
                ins = e.matmul(psS(ps), lhs_fn(k), rhs[:, TP:T], start=st, stop=sp_)
            return ins
        S.add("pe", fn, r=rkeys, w=[f"ps{ps}"])

    def mm_kouter(groups, rhs_fn, nk, xkeys):
        for k in range(nk):
            def fn(e, k=k):
                ins = None
                rhs = rhs_fn(k)
                for (ps, lhs_fn, _) in groups:
                    e.matmul(psP(ps), lhs_fn(k), rhs[:, 0:TP], start=(k == 0), stop=(k == nk - 1))
                    ins = e.matmul(psS(ps), lhs_fn(k), rhs[:, TP:T], start=(k == 0), stop=(k == nk - 1))
                return ins
            S.add("pe", fn, r=[g[2] for g in groups] + [xkeys[k]], w=[f"ps{g[0]}" for g in groups])

    def act2(ps, outP, outS, func, wkeys, rkeys=(), s3=False, **kw):
        def fn(e):
            e.activation(out=outP, in_=psP(ps), func=func, **kw)
            inS = psS(ps)
            if s3:
                inS = inS.rearrange("p (s j) -> p s j", j=8)
            return e.activation(out=outS, in_=inS, func=func, **kw)
        S.add("act", fn, r=[f"ps{ps}"] + list(rkeys), w=wkeys)

    def act1(out, in_, func, rkeys, wkeys, **kw):
        S.add("act", lambda e: e.activation(out=out, in_=in_, func=func, **kw), r=rkeys, w=wkeys)

    def dve(fn, rkeys, wkeys):
        S.add("dve", fn, r=rkeys, w=wkeys)

    def s4(ap):
        return ap.rearrange("p (s j) -> p s j", j=8)

    S.add("sp", lambda e: e.dma_start(out=prm_t[:, :, :], in_=prm.rearrange("l p v -> p l v")), w=["prm"], slot="prm")
    S.add("pool", lambda e: e.dma_start(out=identb[:, :], in_=identd[:, :]), w=["identb"], slot="ident")
    dve(lambda e: e.memset(onesb[:, :], 1.0), [], ["onesb"])
    dve(lambda e: e.memset(car_a[:, :, :, :], 0.0), [], ["car_a"])
    dve(lambda e: e.memset(car_b[:, :, :, :], 0.0), [], ["car_b"])
    dve(lambda e: e.memset(car_h[:, :, :], 0.0), [], ["car_h"])
    dve(lambda e: e.memset(car_f[:, :, :, :], 0.0), [], ["car_f"])
    for l in range(nlayer):
        lam = prm_t[:, l, _off["lam"]:_off["lam"] + 16]
        act1(tf(0, 16), lam, AF.Exp, ["prm"], [tk[0]], scale=-1.0)
        act1(tf(1, 16), tf(0, 16), AF.Ln, [tk[0]], [tk[1]], bias=1.0)
        dve(lambda e, l=l: e.tensor_scalar(out=nsp8[:, l, :], in0=tf(1, 16), scalar1=-8.0, scalar2=None, op0=ALU.mult),
            [tk[1]], ["nsp8"])

    def pcol(l, name, i):
        o = _off[name] + i
        return prm_t[:, l, o:o + 1]

    class LN:
        def __init__(self, nch, src_fn, src_key_fn):
            self.nch = nch
            self.src_fn = src_fn
            self.src_key_fn = src_key_fn
            self.pending = None
            self.count = 0

        def feed(self, m):
            ti = 3 + (self.count % 2)
            vb = tb(ti, T, 0)
            vs = tb(ti, T, T)
            src = self.src_fn(m)
            act1(vb, src, AF.Copy, self.src_key_fn(m), [tk[ti]])
            act1(vs, src, AF.Square, self.src_key_fn(m), [tk[ti]])
            first = self.count == 0
            last = self.count == self.nch - 1
            self.count += 1
            self.flush()

            def stat(first=first, last=last, vb=vb, vs=vs, ti=ti):
                def fn(e):
                    e.matmul(psP(2), onesb[:, :], vb[:, 0:TP], start=first, stop=last)
                    e.matmul(psS(2), onesb[:, :], vb[:, TP:T], start=first, stop=last)
                    e.matmul(psP(3), onesb[:, :], vs[:, 0:TP], start=first, stop=last)
                    return e.matmul(psS(3), onesb[:, :], vs[:, TP:T], start=first, stop=last)
                S.add("pe", fn, r=[tk[ti], "onesb"], w=["ps2", "ps3"])
            self.pending = stat

        def flush(self):
            if self.pending is not None:
                self.pending()
                self.pending = None

        def finalize(self, nchan):
            self.flush()
            inv = 1.0 / nchan
            mean, rstd, var = tf(5), tf(6), tf(7)

            def f1(e):
                e.tensor_scalar(out=mean[:, 0:TP], in0=psP(2), scalar1=inv, scalar2=None, op0=ALU.mult)
                return e.tensor_scalar(out=mean[:, TP:T], in0=psS(2), scalar1=inv, scalar2=None,
                                       op0=ALU.mult)
            dve(f1, ["ps2"], [tk[5]])
            dve(lambda e: e.tensor_tensor(out=rstd, in0=mean, in1=mean, op=ALU.mult), [tk[5]], [tk[6]])

            def f2(e):
                e.scalar_tensor_tensor(out=var[:, 0:TP], in0=psP(3), scalar=inv, in1=rstd[:, 0:TP],
                                       op0=ALU.mult, op1=ALU.subtract)
                return e.scalar_tensor_tensor(out=var[:, TP:T], in0=psS(3), scalar=inv,
                                              in1=rstd[:, TP:T], op0=ALU.mult, op1=ALU.subtract)
            dve(f2, ["ps3", tk[6]], [tk[7]])
            dve(lambda e: e.tensor_scalar(out=var, in0=var, scalar1=EPS, scalar2=None, op0=ALU.add), [tk[7]], [tk[7]])
            act1(var, var, AF.Ln, [tk[7]], [tk[7]])
            act1(rstd, var, AF.Exp, [tk[7]], [tk[6]], scale=-0.5)
            return mean, rstd

    def tile_layer(q, l, last_layer):
        S.epoch += 1
        seq0 = NS * q
        tok0 = TP * q

        if l == 0:
            S.add("sp", lambda e: e.dma_start(out=xres[:, :, 0:TP], in_=xpT[:, :, tok0:tok0 + TP]),
                  w=[f"xres{m}" for m in range(16)], slot="xin_p")
            S.add("sp", lambda e: e.dma_start(out=xres[:, :, TP:T], in_=xsT[:, :, TS * q:TS * (q + 1)]),
                  w=[f"xres{m}" for m in range(16)], slot="xin_s")
            for g in range(4):
                gs = slice(4 * g, 4 * g + 4)
                act1(xb[:, gs, :], xres[:, gs, :], AF.Copy, [f"xres{m}" for m in range(4 * g, 4 * g + 4)],
                     [f"xb{m}" for m in range(4 * g, 4 * g + 4)])
        S.add("pool", lambda e: e.dma_start(out=pb[:, :, 0:TP], in_=ppT[l, :, :, tok0:tok0 + TP]), w=["pb"],
              slot="pin_p")
        S.add("pool", lambda e: e.dma_start(out=pb[:, :, TP:T], in_=psT[l, :, :, TS * q:TS * (q + 1)]), w=["pb"],
              slot="pin_s")
        S.add("sp", lambda e: e.dma_start(out=stg_a[:, :, :, :], in_=scaT[l, :, :, seq0:seq0 + NS, :]), w=["stg_a"],
              slot="stg_a")
        S.add("sp", lambda e: e.dma_start(out=stg_b[:, :, :, :], in_=scbT[l, :, :, seq0:seq0 + NS, :]), w=["stg_b"],
              slot="stg_b")
        S.add("sp", lambda e: e.dma_start(out=stg_h[:, :, :], in_=srgT[l, :, :, seq0:seq0 + NS]), w=["stg_h"],
              slot="stg_h")
        S.add("sp", lambda e: e.dma_start(out=stg_f[:, :, :, :], in_=scfT[l, :, :, seq0:seq0 + NS, :]), w=["stg_f"],
              slot="stg_f")
        S.add("sp", lambda e: e.dma_start(out=oas[l, :, :, seq0:seq0 + NS, 0:22],
                                          in_=scaT[l, :, :, seq0:seq0 + NS, 8:30]), slot="o_as0")

        xbk = [f"xb{k}" for k in range(16)]

        def xb_rhs(k):
            return xb[:, k, :]

        ps_allowed[0] = [0, 1]
        lna = LN(8, lambda m: ca32[:, m, slice(0, T)], cak)
        for c in range(8):
            def fdg(e, c=c):
                ins = None
                for k in range(31):
                    ins = e.tensor_scalar(out=dg[:, k, :], in0=identb[:, :], scalar1=pcol(l, "caw", c * 31 + k),
                                          scalar2=None, op0=ALU.mult)
                return ins
            dve(fdg, ["identb", "prm"], ["dg"])
            sv, kv = load_w(W_in[l, c], 2048)
            sg_, kg = load_w(W_in[l, 8 + c], 2048)
            pg = next_ps()
            mm(pg, lambda k, s=sg_: ring[s][:, k * 128:(k + 1) * 128], xb_rhs, 16, [kg] + xbk)
            pv = next_ps()
            mm(pv, lambda k, s=sv: ring[s][:, k * 128:(k + 1) * 128], xb_rhs, 16, [kv] + xbk)
            sg = tf(0)
            act2(pg, sg[:, 0:TP], sg[:, TP:T], AF.Sigmoid, [tk[0]])
            u32 = tf(1)

            def fu(e, pv=pv, sg=sg, u32=u32):
                e.tensor_tensor(out=u32[:, 0:TP], in0=psP(pv), in1=sg[:, 0:TP], op=ALU.mult)
                return e.tensor_tensor(out=u32[:, TP:T], in0=psS(pv), in1=sg[:, TP:T], op=ALU.mult)
            dve(fu, [f"ps{pv}", tk[0]], [tk[1]])
            hp = histp[c % 2]
            hs_ = hists[c % 2]
            hk = f"hist{c % 2}"
            act1(hp[:, 0:30], car_a[:, l, c, :], AF.Copy, ["car_a"], [hk])
            act1(hs_[:, :, 0:30], stg_a[:, c, :, :], AF.Copy, ["stg_a"], [hk])
            act1(hp[:, 30:30 + TP], u32[:, 0:TP], AF.Copy, [tk[1]], [hk])
            act1(hs_[:, :, 30:38], s4(u32[:, TP:T]), AF.Copy, [tk[1]], [hk])
            dve(lambda e, c=c, u32=u32: e.tensor_copy(out=car_a[:, l, c, :], in_=u32[:, TP - 30:TP]), [tk[1], hk],
                ["car_a"])
            dve(lambda e, c=c, u32=u32: e.tensor_copy(out=os_a[:, c, :, :], in_=s4(u32[:, TP:T])), [tk[1]],
                ["os_a"])
            pc = next_ps()

            def fconv(e, hp=hp, hs_=hs_, pc=pc):
                ins = None
                for k in range(31):
                    e.matmul(psP(pc), dg[:, k, :], hp[:, k:k + TP], start=(k == 0), stop=(k == 30))
                for k in range(31):
                    ins = e.matmul(psS(pc).rearrange("p (s j) -> p s j", j=8), dg[:, k, :],
                                   hs_[:, :, k:k + 8], start=(k == 0), stop=(k == 30))
                return ins
            S.add("pe", fconv, r=["dg", hk], w=[f"ps{pc}"])
            act2(pc, ca32[:, c, slice(0, TP)], ca32[:, c, slice(TP, T)], AF.Identity, cak(c), ["prm"],
                 bias=pcol(l, "cab", c), scale=1.0)
            lna.feed(c)
        mean, rstd = lna.finalize(DC)
        for c in range(8):
            ti = (2, 0, 1)[c % 3]
            tn = tf(ti)
            dve(lambda e, c=c, tn=tn: e.tensor_tensor(out=tn, in0=ca32[:, c, slice(0, T)], in1=mean,
                                                       op=ALU.subtract), cak(c) + [tk[5]], [tk[ti]])
            dve(lambda e, tn=tn: e.tensor_tensor(out=tn, in0=tn, in1=rstd, op=ALU.mult), [tk[ti], tk[6]], [tk[ti]])
            act1(work[:, c, :], tn, AF.Silu, [tk[ti], "prm"], [f"work{c}"], scale=pcol(l, "lag", c),
                 bias=pcol(l, "lab", c))

        ps_allowed[0] = [0, 1, 2, 3]
        sAk = [f"work{k}" for k in range(8)]
        for m in range(16):
            s1, k1 = load_w(W_in[l, 48 + m], 2048)
            s2, k2 = load_w(W_aout[l, m], 1024)
            p1 = next_ps()
            mm(p1, lambda k, s=s1: ring[s][:, k * 128:(k + 1) * 128], xb_rhs, 16, [k1] + xbk)
            p2 = next_ps()
            mm(p2, lambda k, s=s2: ring[s][:, k * 128:(k + 1) * 128], lambda k: work[:, k, :], 8, [k2] + sAk)
            sg = tf(m % 2)
            act2(p1, sg[:, 0:TP], sg[:, TP:T], AF.Sigmoid, [tk[m % 2]])

            def fm(e, p2=p2, sg=sg, m=m):
                e.tensor_tensor(out=work[:, 24 + m, 0:TP], in0=psP(p2), in1=sg[:, 0:TP], op=ALU.mult)
                return e.tensor_tensor(out=work[:, 24 + m, TP:T], in0=psS(p2), in1=sg[:, TP:T],
                                       op=ALU.mult)
            dve(fm, [f"ps{p2}", tk[m % 2]], [f"work{24 + m}"])

        sgt, kgt = None, None
        S.add("pool", lambda e: e.dma_start(out=wgate[:, 0, :], in_=W_gate[l, 0], max_dma_last_dim=8192),
              w=["wgate"], slot="wgate")
        S.add("pool", lambda e: e.dma_start(out=wgate[:, 1, :], in_=W_gate[l, 1], max_dma_last_dim=8192),
              w=["wgate"], slot="wgate")
        st = {}

        def m2_pe_proj(c):
            sx, kx = load_w(W_in[l, 16 + c], 2048)
            sgl, kgl = load_w(W_in[l, 32 + c], 2048)
            px = next_ps()
            mm(px, lambda k, s=sx: ring[s][:, k * 128:(k + 1) * 128], xb_rhs, 16, [kx] + xbk)
            pgt = next_ps()
            mm(pgt, lambda k, s=sgl: ring[s][:, k * 128:(k + 1) * 128], xb_rhs, 16, [kgl] + xbk)
            st[c] = (px, pgt)

        def m2_rest_a(c):
            px, pgt = st[c]
            hi = c % 2
            hb = tmp[hi]
            hbs = tmp[hi][:, 520:520 + NS * 11].rearrange("p (s j) -> p s j", j=11)
            dve(lambda e: e.tensor_copy(out=hb[:, 0:3], in_=car_b[:, l, c, :]), ["car_b"], [tk[hi]])
            dve(lambda e: e.tensor_copy(out=hbs[:, :, 0:3], in_=stg_b[:, c, :, :]), ["stg_b"], [tk[hi]])

            def fev(e):
                e.tensor_copy(out=hb[:, 3:3 + TP], in_=psP(px))
                return e.tensor_copy(out=hbs[:, :, 3:11], in_=psS(px).rearrange("p (s j) -> p s j", j=8))
            dve(fev, [f"ps{px}"], [tk[hi]])
            act2(pgt, work[:, 8 + c, 0:TP], work[:, 8 + c, TP:T], AF.Gelu_apprx_tanh, [f"work{8 + c}"])
            dve(lambda e: e.tensor_copy(out=car_b[:, l, c, :], in_=hb[:, TP:TP + 3]), [tk[hi]], ["car_b"])
            dve(lambda e: e.tensor_copy(out=os_b[:, c, :, :], in_=hbs[:, :, 8:11]), [tk[hi]], ["os_b"])
            hbb = tb(2, 564, 568 * hi)
            dve(lambda e: e.tensor_copy(out=hbb, in_=hb[:, 0:564]), [tk[hi]], [f"hbb{hi}"])

            def fdgb(e):
                ins = None
                for k in range(4):
                    ins = e.tensor_scalar(out=dgb[hi][:, k, :], in0=identb[:, :], scalar1=pcol(l, "cbw", c * 4 + k),
                                          scalar2=None, op0=ALU.mult)
                return ins
            dve(fdgb, ["identb", "prm"], [f"dgb{hi}"])

        def m2_b(c):
            hi = c % 2
            tB = [3, 4, 5, 6, 7] if hi == 0 else [8, 9, 10, 11, 12]
            hbb = tb(2, 564, 568 * hi)
            hbbs = hbb[:, 520:520 + NS * 11].rearrange("p (s j) -> p s j", j=11)
            pcv = next_ps()

            def fconv4(e):
                ins = None
                for k in range(4):
                    e.matmul(psP(pcv), dgb[hi][:, k, :], hbb[:, k:k + TP], start=(k == 0), stop=(k == 3))
                for k in range(4):
                    ins = e.matmul(psS(pcv).rearrange("p (s j) -> p s j", j=8), dgb[hi][:, k, :],
                                   hbbs[:, :, k:k + 8], start=(k == 0), stop=(k == 3))
                return ins
            S.add("pe", fconv4, r=[f"hbb{hi}", f"dgb{hi}"], w=[f"ps{pcv}"])
            cbb = tb(hi)
            act2(pcv, cbb[:, 0:TP], cbb[:, TP:T], AF.Identity, [tk[hi]], ["prm"], bias=pcol(l, "cbb", c), scale=1.0)
            cb = tf(tB[4])

            act2(pcv, cb[:, 0:TP], cb[:, TP:T], AF.Identity, [tk[tB[4]]], ["prm"], bias=pcol(l, "cbb", c), scale=1.0)
            pr = next_ps()
            mm(pr, lambda k: wgate[:, 0, c * 128:(c + 1) * 128], lambda k: cbb, 1, ["wgate", tk[hi]])
            pi = next_ps()
            mm(pi, lambda k: wgate[:, 1, c * 128:(c + 1) * 128], lambda k: cbb, 1, ["wgate", tk[hi]])
            r_ = tf(tB[0])
            gi = tf(tB[1])
            act2(pr, r_[:, 0:TP], r_[:, TP:T], AF.Sigmoid, [tk[tB[0]]], ["prm"], bias=pcol(l, "br", c), scale=1.0)
            act2(pi, gi[:, 0:TP], gi[:, TP:T], AF.Sigmoid, [tk[tB[1]]], ["prm"], bias=pcol(l, "bi", c), scale=1.0)
            act1(r_, r_, AF.Exp, [tk[tB[0]], "nsp8"], [tk[tB[0]]], scale=nsp8[:, l, c:c + 1])
            om = tf(tB[2])
            dve(lambda e: e.tensor_tensor(out=om, in0=r_, in1=r_, op=ALU.mult), [tk[tB[0]]], [tk[tB[2]]])
            dve(lambda e: e.tensor_scalar(out=om, in0=om, scalar1=-1.0, scalar2=1.0, op0=ALU.mult, op1=ALU.add),
                [tk[tB[2]]], [tk[tB[2]]])
            act1(om, om, AF.Sqrt, [tk[tB[2]]], [tk[tB[2]]])
            if q == 0:
                dve(lambda e: e.memset(om[:, 0:1], 1.0), [tk[tB[2]]], [tk[tB[2]]])

            dve(lambda e: e.tensor_tensor(out=gi, in0=gi, in1=cb, op=ALU.mult), [tk[tB[1]], tk[tB[4]]], [tk[tB[1]]])
            dve(lambda e: e.tensor_tensor(out=om, in0=gi, in1=om, op=ALU.mult), [tk[tB[1]], tk[tB[2]]], [tk[tB[2]]])
            hs = tf(tB[3])

            def fscan(e):
                e.tensor_tensor_scan(out=hs[:, 0:TP], data0=r_[:, 0:TP], data1=om[:, 0:TP],
                                     initial=car_h[:, l, c:c + 1], op0=ALU.mult, op1=ALU.add)
                ins = None
                for s_ in range(NS):
                    a0 = TP + 8 * s_
                    ins = e.tensor_tensor_scan(out=hs[:, a0:a0 + 8], data0=r_[:, a0:a0 + 8], data1=om[:, a0:a0 + 8],
                                               initial=stg_h[:, c, s_:s_ + 1], op0=ALU.mult, op1=ALU.add)
                return ins
            dve(fscan, [tk[tB[0]], tk[tB[2]], "car_h", "stg_h"], [tk[tB[3]]])
            dve(lambda e: e.tensor_copy(out=car_h[:, l, c:c + 1], in_=hs[:, TP - 1:TP]), [tk[tB[3]]], ["car_h"])
            dve(lambda e: e.tensor_copy(out=os_h[:, c, :], in_=s4(hs[:, TP:T])[:, :, 7]), [tk[tB[3]]], ["os_h"])
            dve(lambda e: e.tensor_tensor(out=work[:, 8 + c, :], in0=hs, in1=work[:, 8 + c, :], op=ALU.mult),
                [tk[tB[3]], f"work{8 + c}"], [f"work{8 + c}"])

        m2_pe_proj(0)
        m2_rest_a(0)
        for c in range(16):
            if c + 1 < 16:
                m2_pe_proj(c + 1)
                m2_rest_a(c + 1)
            m2_b(c)

        hgk = [f"work{8 + k}" for k in range(16)]
        for m in range(16):
            s1, k1 = load_w(W_in[l, 64 + m], 2048)
            s2, k2 = load_w(W_bout[l, m], 2048)
            p1 = next_ps()
            mm(p1, lambda k, s=s1: ring[s][:, k * 128:(k + 1) * 128], xb_rhs, 16, [k1] + xbk)
            p2 = next_ps()
            mm(p2, lambda k, s=s2: ring[s][:, k * 128:(k + 1) * 128], lambda k: work[:, 8 + k, :], 16, [k2] + hgk)
            sg = tf(m % 2)
            act2(p1, sg[:, 0:TP], sg[:, TP:T], AF.Sigmoid, [tk[m % 2]])
            t2 = tf(2 + m % 2)

            def fm(e, p2=p2, sg=sg, t2=t2):
                e.tensor_tensor(out=t2[:, 0:TP], in0=psP(p2), in1=sg[:, 0:TP], op=ALU.mult)
                return e.tensor_tensor(out=t2[:, TP:T], in0=psS(p2), in1=sg[:, TP:T], op=ALU.mult)
            dve(fm, [f"ps{p2}", tk[m % 2]], [tk[2 + m % 2]])
            dve(lambda e, m=m, t2=t2: e.tensor_tensor(out=work[:, 24 + m, :], in0=work[:, 24 + m, :], in1=t2,
                                                       op=ALU.add), [tk[2 + m % 2], f"work{24 + m}"],
                [f"work{24 + m}"])

        def residual_ln(wsrc, nk, rhs_fn, rkeys, gname, bname, extra=None):
            ps_allowed[0] = [0, 1]
            ln = LN(16, lambda m: xres[:, m, :], lambda m: [f"xres{m}"])
            for m in range(16):
                if extra is None:
                    s1, k1 = load_w(wsrc(m), nk * 128)
                    p1 = next_ps()
                    mm(p1, lambda k, s=s1: ring[s][:, k * 128:(k + 1) * 128], rhs_fn, nk, [k1] + rkeys)
                else:
                    p1 = extra(m)

                def fr(e, p1=p1, m=m):
                    e.scalar_tensor_tensor(out=xres[:, m, 0:TP], in0=xres[:, m, 0:TP], scalar=ALPHA,
                                           in1=psP(p1), op0=ALU.mult, op1=ALU.add)
                    return e.scalar_tensor_tensor(out=xres[:, m, TP:T], in0=xres[:, m, TP:T], scalar=ALPHA,
                                                  in1=psS(p1), op0=ALU.mult, op1=ALU.add)
                dve(fr, [f"ps{p1}", f"xres{m}"], [f"xres{m}"])
                ln.feed(m)
            mean, rstd = ln.finalize(D)
            for m in range(16):
                ti = (2, 0, 1)[m % 3]
                tn = tf(ti)
                dve(lambda e, m=m, tn=tn: e.tensor_tensor(out=tn, in0=xres[:, m, :], in1=mean, op=ALU.subtract),
                    [f"xres{m}", tk[5]], [tk[ti]])
                dve(lambda e, tn=tn: e.tensor_tensor(out=tn, in0=tn, in1=rstd, op=ALU.mult), [tk[ti], tk[6]],
                    [tk[ti]])
                act1(xres[:, m, :], tn, AF.Identity, [tk[ti], "prm"], [f"xres{m}"], scale=pcol(l, gname, m),
                     bias=pcol(l, bname, m))
                act1(xb[:, m, :], xres[:, m, :], AF.Copy, [f"xres{m}"], [f"xb{m}"])
            ps_allowed[0] = [0, 1, 2, 3]

        mgk = [f"work{24 + k}" for k in range(16)]
        residual_ln(lambda m: W_o[l, m], 16, lambda k: work[:, 24 + k, :], mgk, "l1g", "l1b")

        ffn_pre = {}
        gl = []
        for j in (0, 1):
            sgw, kgw = load_w(W_up[l, 48 + j], 2048)
            suw, kuw = load_w(W_up[l, j], 2048)
            pgp = next_ps()
            pup = next_ps()
            gl.append((pgp, lambda k, s=sgw: ring[s][:, k * 128:(k + 1) * 128], kgw))
            gl.append((pup, lambda k, s=suw: ring[s][:, k * 128:(k + 1) * 128], kuw))
            ffn_pre[j] = (pgp, pup)
        mm_kouter(gl, xb_rhs, 16, xbk)
        for j in range(48):
            if j in ffn_pre:
                pgp, pup = ffn_pre[j]
            else:
                sgw, kgw = load_w(W_up[l, 48 + j], 2048)
                suw, kuw = load_w(W_up[l, j], 2048)
                pgp = next_ps()
                mm(pgp, lambda k, s=sgw: ring[s][:, k * 128:(k + 1) * 128], xb_rhs, 16, [kgw] + xbk)
                pup = next_ps()
                mm(pup, lambda k, s=suw: ring[s][:, k * 128:(k + 1) * 128], xb_rhs, 16, [kuw] + xbk)
            hi = j % 2
            hf = tmp[hi]
            hfs = tmp[hi][:, 520:520 + NS * 10].rearrange("p (s j) -> p s j", j=10)
            dve(lambda e, j=j, hf=hf: e.tensor_copy(out=hf[:, 0:2], in_=car_f[:, l, j, :]), ["car_f"], [tk[hi]])
            dve(lambda e, j=j, hfs=hfs: e.tensor_copy(out=hfs[:, :, 0:2], in_=stg_f[:, j, :, :]), ["stg_f"], [tk[hi]])
            act2(pgp, hf[:, 2:2 + TP], hfs[:, :, 2:10], AF.Identity, [tk[hi]], s3=True)
            dve(lambda e, j=j, hf=hf: e.tensor_copy(out=car_f[:, l, j, :], in_=hf[:, TP:TP + 2]), [tk[hi]], ["car_f"])
            dve(lambda e, j=j, hfs=hfs: e.tensor_copy(out=os_f[:, j, :, :], in_=hfs[:, :, 8:10]), [tk[hi]], ["os_f"])
            fc = tf(2 + hi)

            def ffc(e, j=j, hf=hf, hfs=hfs, fc=fc):
                fcs = s4(fc[:, TP:T])
                e.tensor_scalar(out=fc[:, 0:TP], in0=hf[:, 0:TP], scalar1=pcol(l, "fcw", j * 3),
                                scalar2=pcol(l, "fcb", j), op0=ALU.mult, op1=ALU.add)
                e.tensor_scalar(out=fcs, in0=hfs[:, :, 0:8], scalar1=pcol(l, "fcw", j * 3),
                                scalar2=pcol(l, "fcb", j), op0=ALU.mult, op1=ALU.add)
                ins = None
                for k in range(1, 3):
                    e.scalar_tensor_tensor(out=fc[:, 0:TP], in0=hf[:, k:k + TP], scalar=pcol(l, "fcw", j * 3 + k),
                                           in1=fc[:, 0:TP], op0=ALU.mult, op1=ALU.add)
                    ins = e.scalar_tensor_tensor(out=fcs, in0=hfs[:, :, k:k + 8], scalar=pcol(l, "fcw", j * 3 + k),
                                                 in1=fcs, op0=ALU.mult, op1=ALU.add)
                return ins
            dve(ffc, [tk[hi], "prm"], [tk[2 + hi]])
            act1(fc, fc, AF.Gelu_apprx_tanh, [tk[2 + hi]], [tk[2 + hi]])

            def fh(e, j=j, pup=pup, fc=fc):
                e.tensor_tensor(out=work[:, j, 0:TP], in0=psP(pup), in1=fc[:, 0:TP], op=ALU.mult)
                return e.tensor_tensor(out=work[:, j, TP:T], in0=psS(pup), in1=fc[:, TP:T], op=ALU.mult)
            dve(fh, [f"ps{pup}", tk[2 + hi]], [f"work{j}"])

        hk_all = [f"work{k}" for k in range(48)]

        def down_extra(m):
            sa_, ka = load_w(W_down[l, 2 * m], 24 * 128)
            sb_, kb = load_w(W_down[l, 2 * m + 1], 24 * 128)
            p1 = next_ps()
            mm(p1, lambda k, s=sa_: ring[s][:, k * 128:(k + 1) * 128], lambda k: work[:, k, :], 24, [ka] + hk_all[:24],
               first=True, last=False)
            mm(p1, lambda k, s=sb_: ring[s][:, k * 128:(k + 1) * 128], lambda k: work[:, 24 + k, :], 24,
               [kb] + hk_all[24:], first=False, last=True)
            return p1
        residual_ln(None, 0, None, None, "l2g", "l2b", extra=down_extra)

        def ple_extra(m):
            s1, k1 = load_w(W_pg[l, m], 2048)
            s2, k2 = load_w(W_pe[l, m], 256)
            p1 = next_ps()
            mm(p1, lambda k, s=s1: ring[s][:, k * 128:(k + 1) * 128], xb_rhs, 16, [k1] + xbk)
            sg = tf(m % 2)
            act2(p1, sg[:, 0:TP], sg[:, TP:T], AF.Sigmoid, [tk[m % 2]], ["prm"], bias=pcol(l, "bpg", m), scale=1.0)
            p2 = next_ps()
            mm(p2, lambda k, s=s2: ring[s][:, k * 128:(k + 1) * 128], lambda k: pb[:, k, :], 2, [k2, "pb"])
            def fe(e, p2=p2, sg=sg):
                e.tensor_tensor(out=sg[:, 0:TP], in0=psP(p2), in1=sg[:, 0:TP], op=ALU.mult)
                return e.tensor_tensor(out=sg[:, TP:T], in0=psS(p2), in1=sg[:, TP:T], op=ALU.mult)
            dve(fe, [f"ps{p2}", tk[m % 2]], [tk[m % 2]])
            return ("sb", m % 2)
        ps_allowed[0] = [0, 1]
        ln = LN(16, lambda m: xres[:, m, :], lambda m: [f"xres{m}"])
        for m in range(16):
            _, ti = ple_extra(m)
            dve(lambda e, m=m, ti=ti: e.scalar_tensor_tensor(out=xres[:, m, :], in0=xres[:, m, :], scalar=ALPHA,
                                                              in1=tf(ti), op0=ALU.mult, op1=ALU.add),
                [tk[ti], f"xres{m}"], [f"xres{m}"])
            ln.feed(m)
        mean, rstd = ln.finalize(D)
        for m in range(16):
            ti = (2, 0, 1)[m % 3]
            tn = tf(ti)
            dve(lambda e, m=m, tn=tn: e.tensor_tensor(out=tn, in0=xres[:, m, :], in1=mean, op=ALU.subtract),
                [f"xres{m}", tk[5]], [tk[ti]])
            dve(lambda e, tn=tn: e.tensor_tensor(out=tn, in0=tn, in1=rstd, op=ALU.mult), [tk[ti], tk[6]], [tk[ti]])
            act1(xres[:, m, :], tn, AF.Identity, [tk[ti], "prm"], [f"xres{m}"], scale=pcol(l, "l3g", m),
                 bias=pcol(l, "l3b", m))
            if not last_layer:
                act1(xb[:, m, :], xres[:, m, :], AF.Copy, [f"xres{m}"], [f"xb{m}"])
        ps_allowed[0] = [0, 1, 2, 3]

        xk = [f"xres{m}" for m in range(16)]
        if last_layer:
            S.add("sp", lambda e: e.dma_start(out=ypT[:, :, tok0:tok0 + TP], in_=xres[:, :, 0:TP]), r=xk, slot="yo_p")
            S.add("sp", lambda e: e.dma_start(out=ysT[:, :, TS * q:TS * (q + 1)], in_=xres[:, :, TP:T]), r=xk,
                  slot="yo_s")
        S.add("sp", lambda e: e.dma_start(out=oas[l, :, :, seq0:seq0 + NS, 22:30], in_=os_a[:, :, :, :]), r=["os_a"],
              slot="o_as")
        S.add("sp", lambda e: e.dma_start(out=obs[l, :, :, seq0:seq0 + NS, :], in_=os_b[:, :, :, :]), r=["os_b"],
              slot="o_bs")
        S.add("sp", lambda e: e.dma_start(out=ohs[l, :, :, seq0:seq0 + NS], in_=os_h[:, :, :]), r=["os_h"],
              slot="o_hs")
        S.add("sp", lambda e: e.dma_start(out=ofs[l, :, :, seq0:seq0 + NS, :], in_=os_f[:, :, :, :]), r=["os_f"],
              slot="o_fs")
        if q == ntile - 1:
            S.add("sp", lambda e: e.dma_start(out=oap[l, :, :, :], in_=car_a[:, l, :, :]), r=["car_a"], slot="o_ap")
            S.add("sp", lambda e: e.dma_start(out=obp[l, :, :, :], in_=car_b[:, l, :, :]), r=["car_b"], slot="o_bp")
            S.add("sp", lambda e: e.dma_start(out=ohp[l, :, :], in_=car_h[:, l, :]), r=["car_h"], slot="o_hp")
            S.add("sp", lambda e: e.dma_start(out=ofp[l, :, :, :], in_=car_f[:, l, :, :]), r=["car_f"], slot="o_fp")

    for q in range(ntile):
        for l in range(nlayer):
            tile_layer(q, l, l == nlayer - 1)

    final = [s for s in S.slot_cnt if s.startswith("o_") or s.startswith("yo_")]
    S.emit(nc, stack, final)
    stack.close()
    return nc


def _panels(w, kc):
    K, N = w.shape
    assert K == kc * 128
    return np.ascontiguousarray(w.reshape(kc, 128, N // 128, 128).transpose(2, 1, 0, 3)).reshape(N // 128, 128,
                                                                                                 kc * 128)


def _vec(v):
    return v.reshape(-1, 128).T


_NC_CACHE = {}
_PREP_ONLY = [False]


def kernel(x_prompt, x_sample, state_conv_a, state_conv_b, state_rglru, state_conv_ffn, p_prompt, p_sample,
           w_in, conv_a_w, conv_a_b, ln_a_g, ln_a_b, w_a_out, conv_b_w, conv_b_b, w_r, b_r, w_i, b_i,
           lru_lambda, w_b_out, w_o, ln1_g, ln1_b, w_up, ffn_conv_w, ffn_conv_b, w_down, ln2_g, ln2_b,
           w_pe, w_pg, b_pg, ln3_g, ln3_b):
    f = lambda a: np.asarray(a, dtype=np.float32)
    (x_prompt, x_sample, state_conv_a, state_conv_b, state_rglru, state_conv_ffn, p_prompt, p_sample, w_in,
     conv_a_w, conv_a_b, ln_a_g, ln_a_b, w_a_out, conv_b_w, conv_b_b, w_r, b_r, w_i, b_i, lru_lambda, w_b_out, w_o,
     ln1_g, ln1_b, w_up, ffn_conv_w, ffn_conv_b, w_down, ln2_g, ln2_b, w_pe, w_pg, b_pg, ln3_g, ln3_b) = map(f, (
         x_prompt, x_sample, state_conv_a, state_conv_b, state_rglru, state_conv_ffn, p_prompt, p_sample, w_in,
         conv_a_w, conv_a_b, ln_a_g, ln_a_b, w_a_out, conv_b_w, conv_b_b, w_r, b_r, w_i, b_i, lru_lambda, w_b_out,
         w_o, ln1_g, ln1_b, w_up, ffn_conv_w, ffn_conv_b, w_down, ln2_g, ln2_b, w_pe, w_pg, b_pg, ln3_g, ln3_b))

    shared = {
        "W_in": np.stack([_panels(w_in[l], 16) for l in range(L)]),
        "W_aout": np.stack([_panels(w_a_out[l], 8) for l in range(L)]),
        "W_bout": np.stack([_panels(w_b_out[l], 16) for l in range(L)]),
        "W_o": np.stack([_panels(w_o[l], 16) for l in range(L)]),
        "W_up": np.stack([_panels(w_up[l], 16) for l in range(L)]),
        "W_pg": np.stack([_panels(w_pg[l], 16) for l in range(L)]),
        "W_pe": np.stack([_panels(w_pe[l], 2) for l in range(L)]),
        "ident": np.eye(128, dtype=np.float32),
    }
    wd = []
    for l in range(L):
        a = _panels(w_down[l][:3072], 24)
        b = _panels(w_down[l][3072:], 24)
        wd.append(np.stack([a, b], axis=1).reshape(32, 128, 24 * 128))
    shared["W_down"] = np.stack(wd)
    wg = []
    for l in range(L):
        wg.append(np.stack([np.ascontiguousarray(w_r[l].transpose(1, 0, 2)).reshape(128, 2048),
                            np.ascontiguousarray(w_i[l].transpose(1, 0, 2)).reshape(128, 2048)]))
    shared["W_gate"] = np.stack(wg)
    prm = np.zeros((L, 128, NV), np.float32)
    for l in range(L):
        P = prm[l]
        P[:, _off["caw"]:_off["caw"] + 248] = conv_a_w[l].reshape(31, 8, 128).transpose(2, 1, 0).reshape(128, 248)
        P[:, _off["cab"]:_off["cab"] + 8] = _vec(conv_a_b[l])
        P[:, _off["lag"]:_off["lag"] + 8] = _vec(ln_a_g[l])
        P[:, _off["lab"]:_off["lab"] + 8] = _vec(ln_a_b[l])
        P[:, _off["cbw"]:_off["cbw"] + 64] = conv_b_w[l].reshape(4, 16, 128).transpose(2, 1, 0).reshape(128, 64)
        P[:, _off["cbb"]:_off["cbb"] + 16] = _vec(conv_b_b[l])
        P[:, _off["br"]:_off["br"] + 16] = _vec(b_r[l])
        P[:, _off["bi"]:_off["bi"] + 16] = _vec(b_i[l])
        P[:, _off["lam"]:_off["lam"] + 16] = _vec(lru_lambda[l])
        P[:, _off["l1g"]:_off["l1g"] + 16] = _vec(ln1_g[l])
        P[:, _off["l1b"]:_off["l1b"] + 16] = _vec(ln1_b[l])
        P[:, _off["fcw"]:_off["fcw"] + 144] = ffn_conv_w[l].reshape(3, 48, 128).transpose(2, 1, 0).reshape(128, 144)
        P[:, _off["fcb"]:_off["fcb"] + 48] = _vec(ffn_conv_b[l])
        P[:, _off["l2g"]:_off["l2g"] + 16] = _vec(ln2_g[l])
        P[:, _off["l2b"]:_off["l2b"] + 16] = _vec(ln2_b[l])
        P[:, _off["bpg"]:_off["bpg"] + 16] = _vec(b_pg[l])
        P[:, _off["l3g"]:_off["l3g"] + 16] = _vec(ln3_g[l])
        P[:, _off["l3b"]:_off["l3b"] + 16] = _vec(ln3_b[l])
    shared["prm"] = prm

    def cm(a, nch):
        rows = a.reshape(-1, nch, 128)
        return np.ascontiguousarray(rows.transpose(2, 1, 0))

    in_maps = []
    zx = np.zeros((128, 16, 2048), np.float32)
    zp = np.zeros((L, 128, 2, 2048), np.float32)
    for c in range(NCORES):
        sq = slice(16 * c, 16 * c + 16)
        m = dict(shared)
        if c in PROMPT_CORES:
            s = PROMPT_CORES.index(c)
            m["xpT"] = cm(x_prompt[s], 16)
            m["ppT"] = np.stack([cm(p_prompt[l, s], 2) for l in range(L)])
        else:
            m["xpT"] = zx
            m["ppT"] = zp
        m["xsT"] = cm(x_sample[sq], 16)
        m["psT"] = np.stack([cm(p_sample[l, sq], 2) for l in range(L)])
        m["scaT"] = np.stack([cm(state_conv_a[l, sq], 8).reshape(128, 8, 16, 30) for l in range(L)])
        m["scbT"] = np.stack([cm(state_conv_b[l, sq], 16).reshape(128, 16, 16, 3) for l in range(L)])
        m["srgT"] = np.stack([cm(state_rglru[l, sq], 16).reshape(128, 16, 16) for l in range(L)])
        m["scfT"] = np.stack([cm(state_conv_ffn[l, sq], 48).reshape(128, 48, 16, 2) for l in range(L)])
        in_maps.append(m)

    if _PREP_ONLY[0]:
        return in_maps
    if "nc" not in _NC_CACHE:
        _NC_CACHE["nc"] = build_nc()
    nc = _NC_CACHE["nc"]
    res = run_bass_kernel_spmd(nc, in_maps, core_ids=list(range(NCORES)))
    R = res.results
    return _post(R)


def _post(R):

    def tm(a):
        nch = a.shape[1]
        rest = a.shape[2:]
        return np.ascontiguousarray(np.moveaxis(a.reshape(128, nch, -1), 2, 0).transpose(0, 2, 1)).reshape(
            *rest, nch * 128)

    y_prompt = np.stack([tm(R[c]["ypT"]) for c in PROMPT_CORES])
    y_sample = np.concatenate([tm(R[c]["ysT"]).reshape(16, 8, D) for c in range(NCORES)], axis=0)
    na_p = np.stack([np.stack([tm(R[c]["oap"][l]) for c in PROMPT_CORES]) for l in range(L)])
    nb_p = np.stack([np.stack([tm(R[c]["obp"][l]) for c in PROMPT_CORES]) for l in range(L)])
    nh_p = np.stack([np.stack([tm(R[c]["ohp"][l]) for c in PROMPT_CORES]) for l in range(L)])
    nf_p = np.stack([np.stack([tm(R[c]["ofp"][l]) for c in PROMPT_CORES]) for l in range(L)])
    na_s = np.stack([np.concatenate([tm(R[c]["oas"][l]) for c in range(NCORES)], axis=0) for l in range(L)])
    nb_s = np.stack([np.concatenate([tm(R[c]["obs"][l]) for c in range(NCORES)], axis=0) for l in range(L)])
    nh_s = np.stack([np.concatenate([tm(R[c]["ohs"][l]) for c in range(NCORES)], axis=0) for l in range(L)])
    nf_s = np.stack([np.concatenate([tm(R[c]["ofs"][l]) for c in range(NCORES)], axis=0) for l in range(L)])
    outs = (y_prompt, y_sample, na_p, nb_p, nh_p, nf_p, na_s, nb_s, nh_s, nf_s)
    return tuple(np.ascontiguousarray(o, dtype=np.float32) for o in outs)
```
